# Optimizing a Trainium2 kernel written in Bass

```python
import math
import jax, jax.numpy as jnp
from jax import lax
import numpy as np

D_MODEL = 1024
BATCH = 8
SEQ = 2048
DEPTH = 4
DEC_BATCH = 128
DEC_SEQ = 4
PAST_LEN = 16384
PAGE_SIZE = 128

N_MIXERS = 3
EXPAND = 2
D_INNER = EXPAND * D_MODEL
CHUNK = 128
SGU_GROUPS = 8
SGU_GROUP_DIM = D_INNER // SGU_GROUPS
CONV_W = 3
S5_IN = 16
S5_GROUPS = D_INNER // S5_IN
S5_STATE = 64
MEM_LEN = 256
XA_HEADS = 4
XA_HEAD_DIM = 128
XA_DIM = XA_HEADS * XA_HEAD_DIM
EPS = 1e-6

LAYER_KINDS = tuple(i % N_MIXERS for i in range(DEPTH))
KIND_INDEX = tuple(sum(1 for j in range(i) if LAYER_KINDS[j] == LAYER_KINDS[i]) for i in range(DEPTH))
N_A = sum(1 for k in LAYER_KINDS if k == 0)
N_B = sum(1 for k in LAYER_KINDS if k == 1)
N_C = sum(1 for k in LAYER_KINDS if k == 2)

kernel_name = "interleaved_sgu_conv_s5_memxattn_decoder_step"


def rmsnorm(x, g):
    xf = x.astype(jnp.float32)
    y = xf * lax.rsqrt(jnp.mean(xf * xf, axis=-1, keepdims=True) + EPS)
    return (y * g.astype(jnp.float32)).astype(x.dtype)


def memory_kv(mem, g, w_kv):
    b, m, _ = mem.shape
    kv = (rmsnorm(mem, g) @ w_kv).reshape(b, m, 2, XA_HEADS, XA_HEAD_DIM)
    return kv[:, :, 0], kv[:, :, 1]


def cross_attention(q, k, v):
    b, l, _ = q.shape
    qh = q.reshape(b, l, XA_HEADS, XA_HEAD_DIM)
    s = jnp.einsum("blhd,bmhd->bhlm", qh, k).astype(jnp.float32) * (XA_HEAD_DIM ** -0.5)
    p = jax.nn.softmax(s, axis=-1).astype(v.dtype)
    return jnp.einsum("bhlm,bmhd->blhd", p, v).reshape(b, l, XA_DIM)


def spatial_gate_prompt(u, v, ws, bs):
    b, l, _ = v.shape
    vc = v.reshape(b, l // CHUNK, CHUNK, SGU_GROUPS, SGU_GROUP_DIM)
    mixed = jnp.einsum("gts,bcsgd->bctgd", jnp.tril(ws), vc) + bs.T[None, None, :, :, None]
    return u * mixed.reshape(b, l, D_INNER)


def spatial_gate_sample(u, v, ws, bs):
    b, t, _ = v.shape
    vt = v.reshape(b, t, SGU_GROUPS, SGU_GROUP_DIM)
    mixed = jnp.einsum("gts,bsgd->btgd", jnp.tril(ws[:, :t, :t]), vt) + bs[:, :t].T[None, :, :, None]
    return u * mixed.reshape(b, t, D_INNER)


def short_conv(xc, buf, w):
    l = xc.shape[1]
    xp = jnp.concatenate([buf, xc], axis=1)
    y = sum(w[k] * xp[:, k:k + l] for k in range(CONV_W))
    return y, xp[:, l:]


def s5_discretize(lam_re, lam_im, log_dt, b_re, b_im):
    dt = jnp.exp(log_dt)[:, None]
    mag = jnp.exp(lam_re * dt)
    ar = mag * jnp.cos(lam_im * dt)
    ai = mag * jnp.sin(lam_im * dt)
    nr, ni = ar - 1, ai
    den = lam_re * lam_re + lam_im * lam_im
    cr = (nr * lam_re + ni * lam_im) / den
    ci = (ni * lam_re - nr * lam_im) / den
    bbr = cr[..., None] * b_re - ci[..., None] * b_im
    bbi = cr[..., None] * b_im + ci[..., None] * b_re
    return ar, ai, bbr, bbi


def _complex_affine_combine(e1, e2):
    a1r, a1i, b1r, b1i = e1
    a2r, a2i, b2r, b2i = e2
    return (a1r * a2r - a1i * a2i,
            a1r * a2i + a1i * a2r,
            a2r * b1r - a2i * b1i + b2r,
            a2r * b1i + a2i * b1r + b2i)


def s5_block(u, h0r, h0i, ar, ai, bbr, bbi, c_re, c_im):
    bur = jnp.einsum("gpi,blgi->blgp", bbr, u)
    bui = jnp.einsum("gpi,blgi->blgp", bbi, u)
    bur = bur.at[:, 0].add(ar * h0r - ai * h0i)
    bui = bui.at[:, 0].add(ar * h0i + ai * h0r)
    a_r = jnp.broadcast_to(ar, bur.shape)
    a_i = jnp.broadcast_to(ai, bui.shape)
    _, _, hr, hi = lax.associative_scan(_complex_affine_combine, (a_r, a_i, bur, bui), axis=1)
    y = jnp.einsum("gip,blgp->blgi", c_re, hr) - jnp.einsum("gip,blgp->blgi", c_im, hi)
    return y, hr[:, -1].astype(h0r.dtype), hi[:, -1].astype(h0i.dtype)


def s5_mixer(u, h0r, h0i, lam_re, lam_im, log_dt, b_re, b_im, c_re, c_im, d, w_glu, b_glu, blocked):
    b, l, _ = u.shape
    ar, ai, bbr, bbi = s5_discretize(lam_re, lam_im, log_dt, b_re, b_im)
    ug = u.reshape(b, l, S5_GROUPS, S5_IN)
    if blocked:
        uc = ug.reshape(b, l // CHUNK, CHUNK, S5_GROUPS, S5_IN).transpose(1, 0, 2, 3, 4)

        def step(carry, u_blk):
            y_blk, hr_, hi_ = s5_block(u_blk, carry[0], carry[1], ar, ai, bbr, bbi, c_re, c_im)
            return (hr_, hi_), y_blk

        (hr, hi), ys = lax.scan(step, (h0r, h0i), uc)
        y = ys.transpose(1, 0, 2, 3, 4).reshape(b, l, D_INNER)
    else:
        y, hr, hi = s5_block(ug, h0r, h0i, ar, ai, bbr, bbi, c_re, c_im)
        y = y.reshape(b, l, D_INNER)
    y = y + d * u
    g = jax.nn.gelu(y)
    return g * jax.nn.sigmoid(g @ w_glu + b_glu), hr, hi


def setup_inputs(seed: int = 0) -> dict:
    key = jax.random.key(seed)
    ks = iter(jax.random.split(key, 40))
    f32 = jnp.float32
    nrm = lambda shape, scale: jax.random.normal(next(ks), shape, f32) * scale
    x_prompt = nrm((BATCH, SEQ, D_MODEL), 1.0)
    x_sample = nrm((DEC_BATCH, DEC_SEQ, D_MODEL), 1.0)
    mem_prompt = nrm((BATCH, MEM_LEN, D_MODEL), 1.0)
    cache_mem_k = nrm((DEPTH, DEC_BATCH, MEM_LEN, XA_HEADS, XA_HEAD_DIM), 1.0)
    cache_mem_v = nrm((DEPTH, DEC_BATCH, MEM_LEN, XA_HEADS, XA_HEAD_DIM), 1.0)
    state_conv = nrm((N_B, DEC_BATCH, CONV_W - 1, D_INNER), 1.0)
    state_s5_re = nrm((N_C, DEC_BATCH, S5_GROUPS, S5_STATE), 0.5)
    state_s5_im = nrm((N_C, DEC_BATCH, S5_GROUPS, S5_STATE), 0.5)
    norm_g = 1.0 + nrm((DEPTH, D_MODEL), 0.1)
    final_g = 1.0 + nrm((D_MODEL,), 0.1)
    mem_norm_g = 1.0 + nrm((DEPTH, D_MODEL), 0.1)
    w_kv = nrm((DEPTH, D_MODEL, 2 * XA_DIM), D_MODEL ** -0.5)
    w_out = nrm((DEPTH, D_INNER + XA_DIM, D_MODEL), 0.5 * (D_INNER + XA_DIM) ** -0.5)
    w_in_a = nrm((N_A, D_MODEL, 3 * D_INNER + XA_DIM), D_MODEL ** -0.5)
    sgu_norm_g = 1.0 + nrm((N_A, D_INNER), 0.1)
    sgu_w = nrm((N_A, SGU_GROUPS, CHUNK, CHUNK), CHUNK ** -0.5)
    sgu_b = 1.0 + nrm((N_A, SGU_GROUPS, CHUNK), 0.1)
    w_in_b = nrm((N_B, D_MODEL, 4 * D_INNER + XA_DIM), D_MODEL ** -0.5)
    conv_w = nrm((N_B, CONV_W, D_INNER), CONV_W ** -0.5)
    w_in_c = nrm((N_C, D_MODEL, 2 * D_INNER + XA_DIM), D_MODEL ** -0.5)
    s5_lam_re = -0.5 + nrm((N_C, S5_GROUPS, S5_STATE), 0.01)
    s5_lam_im = jnp.pi * jnp.arange(S5_STATE, dtype=f32)[None, None, :] + nrm((N_C, S5_GROUPS, S5_STATE), 0.01)
    s5_log_dt = jax.random.uniform(next(ks), (N_C, S5_GROUPS), f32, minval=math.log(1e-3), maxval=math.log(1e-1))
    s5_b_re = nrm((N_C, S5_GROUPS, S5_STATE, S5_IN), (2 * S5_IN) ** -0.5)
    s5_b_im = nrm((N_C, S5_GROUPS, S5_STATE, S5_IN), (2 * S5_IN) ** -0.5)
    s5_c_re = nrm((N_C, S5_GROUPS, S5_IN, S5_STATE), (2 * S5_STATE) ** -0.5)
    s5_c_im = nrm((N_C, S5_GROUPS, S5_IN, S5_STATE), (2 * S5_STATE) ** -0.5)
    s5_d = nrm((N_C, D_INNER), 1.0)
    w_glu = nrm((N_C, D_INNER, D_INNER), D_INNER ** -0.5)
    b_glu = nrm((N_C, D_INNER), 0.01)
    return {"x_prompt": x_prompt, "x_sample": x_sample, "mem_prompt": mem_prompt,
            "cache_mem_k": cache_mem_k, "cache_mem_v": cache_mem_v, "state_conv": state_conv,
            "state_s5_re": state_s5_re, "state_s5_im": state_s5_im,
            "norm_g": norm_g, "final_g": final_g, "mem_norm_g": mem_norm_g, "w_kv": w_kv, "w_out": w_out,
            "w_in_a": w_in_a, "sgu_norm_g": sgu_norm_g, "sgu_w": sgu_w, "sgu_b": sgu_b,
            "w_in_b": w_in_b, "conv_w": conv_w, "w_in_c": w_in_c,
            "s5_lam_re": s5_lam_re, "s5_lam_im": s5_lam_im, "s5_log_dt": s5_log_dt,
            "s5_b_re": s5_b_re, "s5_b_im": s5_b_im, "s5_c_re": s5_c_re, "s5_c_im": s5_c_im,
            "s5_d": s5_d, "w_glu": w_glu, "b_glu": b_glu}


def reference(x_prompt, x_sample, mem_prompt, cache_mem_k, cache_mem_v, state_conv, state_s5_re, state_s5_im,
              norm_g, final_g, mem_norm_g, w_kv, w_out, w_in_a, sgu_norm_g, sgu_w, sgu_b, w_in_b, conv_w, w_in_c,
              s5_lam_re, s5_lam_im, s5_log_dt, s5_b_re, s5_b_im, s5_c_re, s5_c_im, s5_d, w_glu, b_glu):
    E = D_INNER

    def trunk(x, mem_k, mem_v, conv_bufs, s5_re, s5_im, prompt):
        conv_out, s5r_out, s5i_out, chunk_v_out = [], [], [], []
        for i in range(DEPTH):
            kind, j = LAYER_KINDS[i], KIND_INDEX[i]
            h = rmsnorm(x, norm_g[i])
            if kind == 0:
                u, v, z, q = jnp.split(h @ w_in_a[j], [E, 2 * E, 3 * E], axis=-1)
                v = rmsnorm(v, sgu_norm_g[j])
                if prompt:
                    y = spatial_gate_prompt(u, v, sgu_w[j], sgu_b[j])
                else:
                    y = spatial_gate_sample(u, v, sgu_w[j], sgu_b[j])
                    chunk_v_out.append(v)
            elif kind == 1:
                bg, cg, hv, z, q = jnp.split(h @ w_in_b[j], [E, 2 * E, 3 * E, 4 * E], axis=-1)
                y, buf = short_conv(cg * hv, conv_bufs[j], conv_w[j])
                y = bg * y
                conv_out.append(buf)
            else:
                u, z, q = jnp.split(h @ w_in_c[j], [E, 2 * E], axis=-1)
                y, hr, hi = s5_mixer(u, s5_re[j], s5_im[j], s5_lam_re[j], s5_lam_im[j], s5_log_dt[j],
                                     s5_b_re[j], s5_b_im[j], s5_c_re[j], s5_c_im[j], s5_d[j],
                                     w_glu[j], b_glu[j], prompt)
                s5r_out.append(hr)
                s5i_out.append(hi)
            xa = cross_attention(q, mem_k[i], mem_v[i])
            x = x + jnp.concatenate([y * jax.nn.silu(z), xa], axis=-1) @ w_out[i]
        return rmsnorm(x, final_g), conv_out, s5r_out, s5i_out, chunk_v_out

    kv = [memory_kv(mem_prompt, mem_norm_g[i], w_kv[i]) for i in range(DEPTH)]
    mem_k_prompt = jnp.stack([k for k, _ in kv])
    mem_v_prompt = jnp.stack([v for _, v in kv])
    bp = x_prompt.shape[0]
    conv0 = jnp.zeros((N_B, bp, CONV_W - 1, D_INNER), x_prompt.dtype)
    s50 = jnp.zeros((N_C, bp, S5_GROUPS, S5_STATE), x_prompt.dtype)
    y_prompt, conv_p, s5r_p, s5i_p, _ = trunk(x_prompt, mem_k_prompt, mem_v_prompt, conv0, s50, s50, True)

    y_sample, conv_s, s5r_s, s5i_s, cv_s = trunk(x_sample, cache_mem_k, cache_mem_v, state_conv,
                                                 state_s5_re, state_s5_im, False)
    return (y_prompt, y_sample, mem_k_prompt, mem_v_prompt, jnp.stack(conv_p), jnp.stack(conv_s),
            jnp.stack(s5r_p), jnp.stack(s5i_p), jnp.stack(s5r_s), jnp.stack(s5i_s), jnp.stack(cv_s))
```

```python
import contextlib
import math
import numpy as np
import concourse.bass as bass
import concourse.mybir as mybir
from concourse.bass_utils import run_bass_kernel_spmd

F32 = mybir.dt.float32
BF16 = mybir.dt.bfloat16
I32 = mybir.dt.int32
AF = mybir.ActivationFunctionType
ALU = mybir.AluOpType

D = 1024
E = 2048
SEQ = 2048
NSB = 16
NS = 64
MEM = 256
EPS = 1e-6
KINDS = (0, 1, 2, 0)
KIDX = (0, 0, 0, 1)


class Buf:
    __slots__ = ("name", "w", "r", "excl")

    def __init__(self, name, excl=False):
        self.name = name
        self.w = None
        self.r = []
        self.excl = excl


class Op:
    __slots__ = ("eng", "fn", "deps", "isdma", "sig", "sem", "val")


class Sched:
    ENG = ("pe", "act", "dve", "pool", "sp")
    NDMASEM = 20

    def __init__(self):
        self.q = {e: [] for e in self.ENG}
        self.dma_last = {}
        self.dma_cnt = {}
        self.dma_rr = {e: 0 for e in self.ENG}
        self.last_barrier = None
        self.lsbufs = []

    def buf(self, name, ls=False, excl=False):
        b = Buf(name, excl)
        if ls:
            b.w = self.last_barrier
            self.lsbufs.append(b)
        return b

    def op(self, eng, fn, R=(), W=(), dma=False, extra=()):
        o = Op()
        o.eng, o.fn, o.isdma, o.sig, o.deps = eng, fn, dma, False, []

        def add(d):
            if d is None or d is o or d in o.deps:
                return
            if (not d.isdma) and (not dma) and d.eng == "pe" and eng == "pe":
                return
            o.deps.append(d)

        for d in extra:
            add(d)
        for b in R:
            add(b.w)
            if b.excl:
                for r in b.r:
                    if r.eng != eng:
                        add(r)
        for b in W:
            add(b.w)
            for r in b.r:
                add(r)
        if dma:
            slot = self.dma_rr[eng] % self.NDMASEM
            self.dma_rr[eng] += 1
            key = (eng, slot)
            add(self.dma_last.get(key))
            self.dma_last[key] = o
            self.dma_cnt[key] = self.dma_cnt.get(key, 0) + 1
            o.sem = key
            o.val = 16 * self.dma_cnt[key]
        for b in R:
            b.r.append(o)
        for b in W:
            b.w = o
            b.r = []
        self.q[eng].append(o)
        return o

    def barrier(self, fn):
        extra = []
        for e in self.ENG:
            for o in reversed(self.q[e]):
                if not o.isdma:
                    extra.append(o)
                    break
        extra += list(self.dma_last.values())
        x = self.op("dve", fn, extra=extra)
        self.last_barrier = x
        self.lsbufs = []
        return x

    def emit(self, nc, es, final_ops):
        for e in self.ENG:
            for o in self.q[e]:
                for d in o.deps:
                    d.sig = True
        for o in final_ops:
            o.sig = True
        csem = {e: es.enter_context(nc.semaphore(f"c_{e}")) for e in self.ENG}
        dsem = {k: es.enter_context(nc.semaphore(f"d_{k[0]}_{k[1]}")) for k in self.dma_cnt}
        for e in self.ENG:
            c = 0
            for o in self.q[e]:
                if o.isdma:
                    o.sem = dsem[o.sem]
                else:
                    if o.sig:
                        c += 1
                        o.val = c
                    o.sem = csem[e]
        sched = self

        def run(e, eng):
            waited = {}

            def w(d):
                k = id(d.sem)
                if waited.get(k, 0) < d.val:
                    eng.wait_ge(d.sem, d.val)
                    waited[k] = d.val

            for o in sched.q[e]:
                for d in o.deps:
                    w(d)
                inst = o.fn(eng)
                if o.isdma:
                    inst.then_inc(o.sem, 16)
                elif o.sig:
                    inst.then_inc(o.sem, 1)
            if e == "sp":
                for d in final_ops:
                    w(d)

        with nc.Block() as block:
            @block.tensor
            def _(eng):
                run("pe", eng)

            @block.scalar
            def _(eng):
                run("act", eng)

            @block.vector
            def _(eng):
                run("dve", eng)

            @block.gpsimd
            def _(eng):
                run("pool", eng)

            @block.sync
            def _(eng):
                run("sp", eng)


def build_program(n_layers=4, do_s5=True, blocks=None, s5prep=True):
    nc = bass.Bass("TRN2", target_bir_lowering=False)
    S = Sched()
    es = contextlib.ExitStack()

    def din(name, shape, dt=F32):
        return nc.dram_tensor(name, list(shape), dt, kind="ExternalInput").ap()

    def dout(name, shape, dt=F32):
        return nc.dram_tensor(name, list(shape), dt, kind="ExternalOutput").ap()

    xp = din("xp", [SEQ, D]); xs = din("xs", [NS, D]); mem = din("mem", [MEM, D])
    ck = din("ck", [4, NSB, MEM, 512]); cv = din("cv", [4, NSB, MEM, 512])
    sconv = din("sconv", [NSB * 2, E]); s5r0 = din("s5r0", [NSB, 128, 64]); s5i0 = din("s5i0", [NSB, 128, 64])
    norm_g = din("norm_g", [4, D]); final_g = din("final_g", [1, D]); mem_norm_g = din("mem_norm_g", [4, D])
    w_kv = din("w_kv", [4, D, D]); w_out = din("w_out", [4, E + 512, D])
    w_in_a = din("w_in_a", [2, D, 3 * E + 512]); sgu_norm_g = din("sgu_norm_g", [2, E])
    sgu_wT = din("sgu_wT", [2, 8, 128, 128]); sgu_b = din("sgu_b", [2, 8 * 128])
    w_in_b = din("w_in_b", [D, 4 * E + 512]); conv_w = din("conv_w", [128, 48])
    w_in_c = din("w_in_c", [D, 2 * E + 512])
    lam_re = din("lam_re", [128, 64]); lam_im = din("lam_im", [128, 64]); log_dt = din("log_dt", [1, 128])
    b_re = din("b_re", [128, 1024]); b_im = din("b_im", [128, 1024])
    c_re = din("c_re", [128, 1024]); c_im = din("c_im", [128, 1024])
    s5_d = din("s5_d", [128, 16]); w_glu = din("w_glu", [E, E]); b_glu = din("b_glu", [128, 16])
    c_ident = din("c_ident", [128, 128]); c_triu = din("c_triu", [128, 128]); c_blk16 = din("c_blk16", [128, 128])
    c_m64 = din("c_m64", [64, 64]); c_sel = din("c_sel", [4, 64]); c_mask8 = din("c_mask8", [128, 8])
    y_p = dout("y_p", [SEQ, D]); y_s = dout("y_s", [NS, D])
    mk_o = dout("mk_o", [4, MEM, 512]); mv_o = dout("mv_o", [4, MEM, 512])
    conv_p = dout("conv_p", [2, E]); conv_s = dout("conv_s", [NSB, 2, E])
    s5r_p = dout("s5r_p", [128, 64]); s5i_p = dout("s5i_p", [128, 64])
    s5r_s = dout("s5r_s", [NSB, 128, 64]); s5i_s = dout("s5i_s", [NSB, 128, 64])
    cv_s = dout("cv_s", [2, NS, E])
    bd_d = nc.dram_tensor("bd_d", [16, 128, 1024], BF16, kind="Internal").ap()
    win_d = nc.dram_tensor("win_d", [16, 128, 1024], BF16, kind="Internal").ap()
    wout2_d = nc.dram_tensor("wout2_d", [128, 64 * 256], BF16, kind="Internal").ap()

    final_ops = []
    marks = []
    _CACHE['marks'] = marks

    def mark(lbl):
        marks.append((lbl, len(S.q['pe'])))
    bscr = S.buf("dram_scratch")

    def sb(name, shape, dt):
        return es.enter_context(nc.sbuf_tensor(name, list(shape), dt))

    x_sb = sb("x_sb", [128, 4, D], F32); bx = S.buf("x")
    hT = sb("hT", [128, 8, 512], BF16); bhT = S.buf("hT")
    bhn4 = [S.buf(f"hn{i}") for i in range(4)]
    junk = sb("junk", [128, D], BF16); bjunk = S.buf("junk")
    NW = 4
    wun = [sb(f"wu{i}", [128, 8, 512], BF16) for i in range(NW)]; bw = [S.buf(f"wu{i}") for i in range(NW)]
    FA = 520
    bufA = sb("bufA", [128, 16 * FA], BF16); bA = S.buf("A")
    hn4 = bufA[:, :4 * D].rearrange("p (t d) -> p t d", t=4)
    bufB = sb("bufB", [128, 16 * FA], BF16); bB = S.buf("B")
    qxa = sb("qxa", [128, 4, 512], BF16); bq = S.buf("qxa")
    exP = [(sb(f"exP{i}", [128, 1024], BF16), S.buf(f"exP{i}")) for i in range(2)]
    NROT = 3
    T1 = [(sb(f"t1_{i}", [128, 512], BF16), S.buf(f"t1_{i}")) for i in range(NROT)]
    T2 = [(sb(f"t2_{i}", [128, 512], BF16), S.buf(f"t2_{i}")) for i in range(NROT)]
    F1 = [(sb(f"f1_{i}", [128, 512], F32), S.buf(f"f1_{i}")) for i in range(NROT)]
    (t1, bt1), (t2, bt2), (f1, bf1) = T1[0], T2[0], F1[0]
    roti = [0]

    def rot():
        nonlocal t1, bt1, t2, bt2, f1, bf1
        roti[0] = (roti[0] + 1) % NROT
        (t1, bt1), (t2, bt2), (f1, bf1) = T1[roti[0]], T2[roti[0]], F1[roti[0]]
    kT_all = sb("kT_all", [128, 4, 4, 256], BF16); bkT = S.buf("kT")
    v_all = sb("v_all", [128, 4, 2, 512], BF16); bv = S.buf("v")
    gB = sb("gB", [128, D], F32); bgB = S.buf("gB")
    idf = sb("idf", [128, 128], F32); idb = sb("idb", [128, 128], BF16); bid = S.buf("id")
    onesb = sb("onesb", [128, 128], BF16)
    sm = sb("sm", [128, 64], F32); bsm = S.buf("sm")
    mask8 = sb("mask8", [128, 8], F32)
    neghalf = sb("neghalf", [128, 16], F32)
    blk16 = sb("blk16", [128, 128], F32)
    halo = sb("halo", [128, 16, 2], BF16); bhalo = S.buf("halo")
    C1 = sb("C1", [128, 2, 64], F32); C2 = sb("C2", [128, 2, 64], F32); AI4 = sb("AI4", [128, 2, 64], F32)
    bS5c = S.buf("s5c")
    HH = [sb(f"HH{i}", [128, 6, 64], F32) for i in range(2)]
    bHH = [S.buf(f"HHh{i}") for i in range(2)]; bHS = [S.buf(f"HHs{i}") for i in range(2)]
    CC3 = sb("CC3", [128, 6, 64], F32)
    Pk = sb("Pk", [128, 6, 64], F32); bPk = S.buf("Pk")
    dcol = sb("dcol", [128, 16], F32); bgl = sb("bgl", [128, 16], F32); wcv = sb("wcv", [128, 48], F32)
    LSN = 33792
    LS = sb("LS", [128, LSN], BF16)
    NPS = 7
    ps = [es.enter_context(nc.psum_tensor(f"ps{i}", [128, 512], F32)) for i in range(NPS)]
    bps = [S.buf(f"ps{i}", excl=True) for i in range(NPS)]
    psT = [es.enter_context(nc.psum_tensor(f"psT{i}", [128, 1024], BF16)) for i in range(1)]
    bpsT = [S.buf(f"psT{i}", excl=True) for i in range(1)]
    rr = {"ps": 0, "psT": 0, "w": 0}

    def nps():
        i = rr["ps"] % NPS
        rr["ps"] += 1
        return ps[i], bps[i]

    def npsT():
        i = rr["psT"] % 1
        rr["psT"] += 1
        return psT[i], bpsT[i]

    class Carve:
        def __init__(self, extra=False):
            self.regions = [(LS, LSN)] + ([(bufA, 16 * FA), (bufB, 16 * FA)] if extra else [])
            self.ri = 0
            self.off = 0

        def __call__(self, name, n, dt, parts=128, buf=None):
            k = n if dt == BF16 else 2 * n
            if self.off % 2:
                self.off += 1
            if self.off + k > self.regions[self.ri][1]:
                self.ri += 1
                self.off = 0
                assert self.ri < len(self.regions), (name, "scratch overflow")
            t_ = self.regions[self.ri][0]
            a = t_[:parts, self.off:self.off + k]
            self.off += k
            if dt != BF16:
                a = a.bitcast(dt)
            return a, (buf if buf is not None else S.buf(name, ls=True))

    def barrier():
        x = S.barrier(lambda e: e.memset(sm[:, 63:64], 0.0))
        for b_ in (bA, bB):
            b_.w = x
            b_.r = []

    def dma(eng, out, in_, R=(), W=()):
        return S.op(eng, lambda e: e.dma_start(out=out, in_=in_), R=R, W=W, dma=True)

    def mm(out, lhsT, rhs, start, stop, R, W, **kw):
        return S.op("pe", lambda e: e.matmul(out, lhsT=lhsT, rhs=rhs, start=start, stop=stop, **kw), R=R, W=W)

    def tr(out, in_, ident, R, W):
        return S.op("pe", lambda e: e.transpose(out=out, in_=in_, identity=ident), R=list(R) + [bid], W=W)

    def act(out, in_, func, R, W, eng="act", **kw):
        return S.op(eng, lambda e: e.activation(out=out, in_=in_, func=func, **kw), R=R, W=W)

    def tt(out, in0, in1, op, R, W, eng="dve"):
        return S.op(eng, lambda e: e.tensor_tensor(out=out, in0=in0, in1=in1, op=op), R=R, W=W)

    def ts(out, in0, s1, s2, op0, op1, R, W, eng="dve"):
        if op1 is None:
            return S.op(eng, lambda e: e.tensor_scalar(out=out, in0=in0, scalar1=s1, scalar2=None, op0=op0), R=R, W=W)
        return S.op(eng, lambda e: e.tensor_scalar(out=out, in0=in0, scalar1=s1, scalar2=s2, op0=op0, op1=op1), R=R, W=W)

    def stt(out, in0, scalar, in1, op0, op1, R, W, eng="dve"):
        return S.op(eng, lambda e: e.scalar_tensor_tensor(out=out, in0=in0, scalar=scalar, in1=in1, op0=op0, op1=op1), R=R, W=W)

    def cp(out, in_, R, W, eng="dve"):
        if eng == "act":
            return S.op("act", lambda e: e.activation(out=out, in_=in_, func=AF.Copy), R=R, W=W)
        return S.op(eng, lambda e: e.tensor_copy(out=out, in_=in_), R=R, W=W)

    def memset(ap, val, W, eng="dve"):
        return S.op(eng, lambda e: e.memset(ap, val), W=W)

    def recip(out, in_, R, W):
        return S.op("dve", lambda e: e.reciprocal(out=out, in_=in_), R=R, W=W)

    NUNITS = 96
    wsc = nc.dram_tensor("wsc", [NUNITS, 128, 8 * 512], BF16, kind="Internal").ap()
    wstate = {"id": None, "first": True}
    wkeys = {}
    bwsc = [S.buf(f"wsc{i}") for i in range(NUNITS)]

    def wload(src):
        i = rr["w"] % NW
        rr["w"] += 1
        kt = src.shape[0] // 128
        cols = src.shape[1]
        if wstate["id"] is None:
            dma("pool", wun[i][:, :kt, :cols], src.rearrange("(k p) c -> p k c", p=128), W=[bw[i]])
            return wun[i], bw[i]
        key = (str(src.tensor.name), int(src.offset), tuple(src.shape))
        if wstate["first"]:
            assert key not in wkeys
            wkeys[key] = len(wkeys)
        uid = wkeys[key]
        assert uid < NUNITS
        flat = wun[i][:].rearrange("p k c -> p (k c)")
        if wstate["first"]:
            dma("pool", wun[i][:, :kt, :cols], src.rearrange("(k p) c -> p k c", p=128), W=[bw[i]])
            if cols == 512:
                dma("sp", wsc[uid, :, :kt * 512], flat[:, :kt * 512], R=[bw[i]], W=[bwsc[uid]])
            else:
                dma("sp", wsc[uid].rearrange("p (k c) -> p k c", k=8)[:, :kt, :cols], wun[i][:, :kt, :cols], R=[bw[i]], W=[bwsc[uid]])
        else:
            if cols == 512:
                dma("sp", flat[:, :kt * 512], wsc[uid, :, :kt * 512], R=[bwsc[uid]], W=[bw[i]])
            else:
                dma("sp", wun[i][:, :kt, :cols], wsc[uid].rearrange("p (k c) -> p k c", k=8)[:, :kt, :cols], R=[bwsc[uid]], W=[bw[i]])
        return wun[i], bw[i]

    dma("sp", idf[:], c_ident, W=[bid])
    cp(idb[:], idf[:], R=[bid], W=[bid])
    memset(onesb[:], 1.0, W=[bid])
    memset(neghalf[:], -0.5, W=[bid])
    dma("sp", mask8[:], c_mask8, W=[bid])
    dma("sp", blk16[:], c_blk16, W=[bid])
    dma("sp", dcol[:], s5_d, W=[bid])
    dma("sp", bgl[:], b_glu, W=[bid])
    dma("sp", wcv[:], conv_w, W=[bid])
    memset(halo[:], 0.0, W=[bhalo])

    def rstd(out_ap, ss_ap, n, R, W):
        ts(out_ap, ss_ap, 1.0 / n, EPS, ALU.mult, ALU.add, R=R, W=W)
        shp = list(out_ap.shape)
        tt(out_ap, out_ap, neghalf[:shp[0], :shp[1]], ALU.pow, R=W, W=W, eng="pool")

    def norm_transpose_multi(items, bxsrc, g_full, bg, bdst):
        n = len(items)
        for i, (x_ap, rows, dst) in enumerate(items):
            act(junk[:rows], x_ap, AF.Square, R=[bxsrc], W=[bjunk, bsm], accum_out=sm[:rows, i:i + 1])
        rmax = max(r for _, r, _ in items)
        rstd(sm[:rmax, 8:8 + n], sm[:rmax, 0:n], D, R=[bsm], W=[bsm])
        for i, (x_ap, rows, dst) in enumerate(items):
            stt(hn4[:rows, i, :], x_ap, sm[:rows, 8 + i:9 + i], g_full[:rows, :], ALU.mult, ALU.mult, R=[bxsrc, bsm, bg], W=[bhn4[i]] + ([bA] if i == 0 else []))
        for i, (x_ap, rows, dst) in enumerate(items):
            pt, bpt = npsT()
            for k in range(8):
                tr(pt[:, k * 128:k * 128 + rows], hn4[:rows, i, k * 128:(k + 1) * 128], idb[:rows, :rows], R=[bhn4[i], bA], W=[bpt])
            cp(dst, pt[:].rearrange("p (k t) -> p k t", k=8)[:, :, :rows], R=[bpt], W=[bdst], eng="act")

    barrier()
    cv_ = Carve()
    mem_sb, bmem = cv_("mem_sb", 2 * D, F32)
    memT, bmemT = cv_("memT", 8 * 256, BF16)
    kvst = [cv_(f"kvst{i}", 512, F32) for i in range(2)]
    mem3 = mem_sb.rearrange("p (t d) -> p t d", t=2)
    memT3 = memT.rearrange("p (k m) -> p k m", k=8)
    dma("sp", mem3, mem.rearrange("(t p) d -> p t d", p=128), W=[bmem])
    gm, bgm = cv_("gm", D, F32)
    kvi = 0
    import os
    for l in range(0 if os.environ.get('DBG_SKIP_MEMKV') else max(n_layers, int(os.environ.get('DBG_MEMKV', '0')))):
        dma("sp", gm, mem_norm_g[l:l + 1, :].partition_broadcast(128), W=[bgm])
        STG = int(os.environ.get('DBG_STAGE', '9'))
        norm_transpose_multi([(mem3[:, t, :], 128, memT3[:, :, t * 128:(t + 1) * 128]) for t in range(2)], bmem, gm, bgm, bmemT)
        for half in range(2):
            if STG < 2:
                continue
            wt, bwt = wload(w_kv[l, :, half * 512:(half + 1) * 512])
            if STG < 3:
                continue
            if half == 0:
                for h in range(4):
                    p, bp = nps()
                    for k in range(8):
                        mm(p[:, :256], wt[:, k, h * 128:(h + 1) * 128], memT3[:, k, :], k == 0, k == 7, R=[bwt, bmemT], W=[bp])
                    cp(kT_all[:, l, h, :], p[:, :256], R=[bp], W=[bkT], eng="act")
            if STG < 4:
                continue
            for t in range(2):
                p, bp = nps()
                for k in range(8):
                    mm(p[:], memT3[:, k, t * 128:(t + 1) * 128], wt[:, k, :], k == 0, k == 7, R=[bwt, bmemT], W=[bp])
                st, bst = kvst[kvi % 2]
                kvi += 1
                cp(st, p[:], R=[bp], W=[bst])
                if half == 1:
                    cp(v_all[:, l, t, :], p[:], R=[bp], W=[bv], eng="act")
                dst = (mk_o if half == 0 else mv_o)[l, t * 128:(t + 1) * 128, :]
                if STG >= 5:
                    final_ops.append(dma("sp", dst, st, R=[bst]))

    s5state = {"cur": 0}
    if do_s5 and s5prep and n_layers > 2:
        barrier()
        cv_ = Carve(extra=True)
        ldT, bld = cv_("ldT", 2 * 1024, F32, parts=64)
        ldP, bldP = cv_("ldP", 2 * 1024, F32, parts=64)
        lr, bpr = cv_("lr", 64, F32); li, _ = cv_("li", 64, F32); dtb, _ = cv_("dtb", 64, F32)
        th, _ = cv_("th", 64, F32); kf, _ = cv_("kf", 64, F32); ki, _ = cv_("ki", 64, I32)
        sn, _ = cv_("sn", 64, F32); cs, _ = cv_("cs", 64, F32); mag, _ = cv_("mag", 64, F32)
        w1, _ = cv_("w1", 64, F32); w2, _ = cv_("w2", 64, F32); w3, _ = cv_("w3", 64, F32)
        crr, _ = cv_("crr", 64, F32); cii, _ = cv_("cii", 64, F32)
        Pr, _ = cv_("Pr", 9 * 64, F32); Pi, _ = cv_("Pi", 9 * 64, F32)
        Br, bBr = cv_("Br", 1024, F32); Bi, _ = cv_("Bi", 1024, F32)
        Cr, bCr = cv_("Cr", 1024, F32); Ci, _ = cv_("Ci", 1024, F32)
        Bbr, bBb = cv_("Bbr", 1024, F32); Bbi, _ = cv_("Bbi", 1024, F32)
        Crb, bCb = cv_("Crb", 1024, BF16); Cinb, _ = cv_("Cinb", 1024, BF16)
        Yr, bY = cv_("Yr", 1024, BF16); Yi, _ = cv_("Yi", 1024, BF16)
        wk1, bwk = cv_("wk1", 1024, F32); wk2, _ = cv_("wk2", 1024, F32)
        stg = [cv_(f"stg{i}", 2048, BF16) for i in range(4)]
        stg2 = [cv_(f"stgw{i}", 2048, BF16) for i in range(2)]
        Pr3 = Pr.rearrange("p (k g) -> p k g", k=9); Pi3 = Pi.rearrange("p (k g) -> p k g", k=9)
        R_ = [bpr]

        def loadT(src, dst_of_col, ncol, stride, bdst):
            ld3 = ldT.rearrange("g (h f) -> g h f", h=2)
            nf = src.shape[1]
            dma("sp", ld3[:, :, :nf], src.rearrange("(h g) f -> g h f", h=2), W=[bld])
            ldP4 = ldP[:, :2 * nf].rearrange("g (c h q) -> g c h q", h=2, q=64)
            if stride == 1:
                src4 = ld3[:, :, :nf].rearrange("g h (c q) -> g c h q", q=64)
            else:
                src4 = ld3[:, :, :nf].rearrange("g h (q c) -> g c h q", c=stride)
            cp(ldP4, src4, R=[bld], W=[bldP])
            for c in range(ncol):
                p, bp = nps()
                tr(p[:, :64], ldP[:, c * 128:(c + 1) * 128], idf[:64, :64], R=[bldP], W=[bp])
                cp(dst_of_col(c), p[:, :64], R=[bp], W=[bdst], eng="act")

        loadT(lam_re, lambda c: lr, 1, 1, bpr)
        loadT(lam_im, lambda c: li, 1, 1, bpr)
        Br3 = Br.rearrange("p (g i) -> p g i", i=16); Bi3 = Bi.rearrange("p (g i) -> p g i", i=16)
        Cr3 = Cr.rearrange("p (g j) -> p g j", j=16); Ci3 = Ci.rearrange("p (g j) -> p g j", j=16)
        loadT(b_re, lambda c: Br3[:, :, c], 16, 16, bBr)
        loadT(b_im, lambda c: Bi3[:, :, c], 16, 16, bBr)
        loadT(c_re, lambda c: Cr3[:, :, c], 16, 1, bCr)
        loadT(c_im, lambda c: Ci3[:, :, c], 16, 1, bCr)
        for h in range(2):
            dma("sp", dtb[h * 64:(h + 1) * 64, :], log_dt[:, h * 64:(h + 1) * 64].partition_broadcast(64), W=[bpr])
        act(dtb, dtb, AF.Exp, R=R_, W=R_)
        tt(th, li, dtb, ALU.mult, R=R_, W=R_)
        ts(kf, th, 1.0 / (2 * math.pi), None, ALU.mult, None, R=R_, W=R_)
        cp(ki, kf, R=R_, W=R_)
        cp(kf, ki, R=R_, W=R_)
        stt(th, kf, -2 * math.pi, th, ALU.mult, ALU.add, R=R_, W=R_)
        ts(w1, th, math.pi, -2 * math.pi, ALU.is_gt, ALU.mult, R=R_, W=R_)
        tt(th, th, w1, ALU.add, R=R_, W=R_)
        ts(w1, th, -math.pi, 2 * math.pi, ALU.is_lt, ALU.mult, R=R_, W=R_)
        tt(th, th, w1, ALU.add, R=R_, W=R_)
        act(sn, th, AF.Sin, R=R_, W=R_)
        stt(w1, th, -1.0, th, ALU.mult, ALU.max, R=R_, W=R_)
        ts(w1, w1, -1.0, math.pi / 2, ALU.mult, ALU.add, R=R_, W=R_)
        act(cs, w1, AF.Sin, R=R_, W=R_)
        tt(w1, lr, dtb, ALU.mult, R=R_, W=R_)
        act(mag, w1, AF.Exp, R=R_, W=R_)
        ar = Pr3[:, 1, :]; ai = Pi3[:, 1, :]
        memset(Pr3[:, 0, :], 1.0, W=R_); memset(Pi3[:, 0, :], 0.0, W=R_)
        tt(ar, mag, cs, ALU.mult, R=R_, W=R_)
        tt(ai, mag, sn, ALU.mult, R=R_, W=R_)
        for k in range(2, 9):
            tt(w1, Pr3[:, k - 1, :], ar, ALU.mult, R=R_, W=R_)
            tt(w2, Pi3[:, k - 1, :], ai, ALU.mult, R=R_, W=R_)
            tt(Pr3[:, k, :], w1, w2, ALU.subtract, R=R_, W=R_)
            tt(w1, Pr3[:, k - 1, :], ai, ALU.mult, R=R_, W=R_)
            tt(w2, Pi3[:, k - 1, :], ar, ALU.mult, R=R_, W=R_)
            tt(Pi3[:, k, :], w1, w2, ALU.add, R=R_, W=R_)
        tt(w1, lr, lr, ALU.mult, R=R_, W=R_)
        tt(w2, li, li, ALU.mult, R=R_, W=R_)
        tt(w1, w1, w2, ALU.add, R=R_, W=R_)
        recip(w3, w1, R=R_, W=R_)
        ts(w1, ar, -1.0, None, ALU.add, None, R=R_, W=R_)
        tt(w2, w1, lr, ALU.mult, R=R_, W=R_)
        tt(crr, ai, li, ALU.mult, R=R_, W=R_)
        tt(crr, crr, w2, ALU.add, R=R_, W=R_)
        tt(crr, crr, w3, ALU.mult, R=R_, W=R_)
        tt(w2, ai, lr, ALU.mult, R=R_, W=R_)
        tt(cii, w1, li, ALU.mult, R=R_, W=R_)
        tt(cii, w2, cii, ALU.subtract, R=R_, W=R_)
        tt(cii, cii, w3, ALU.mult, R=R_, W=R_)
        bc = lambda a: a.unsqueeze(2).broadcast_to([128, 64, 16])
        Bbr3 = Bbr.rearrange("p (g i) -> p g i", i=16); Bbi3 = Bbi.rearrange("p (g i) -> p g i", i=16)
        wk13 = wk1.rearrange("p (g i) -> p g i", i=16); wk23 = wk2.rearrange("p (g i) -> p g i", i=16)
        RB = [bpr, bBr, bBb, bwk]

        def cmul(outr, outi, pr, pi, xr, xi, Rl, Wl, negi=False, gs=slice(0, 64)):
            ng = gs.stop - gs.start
            bcg = lambda a: a[:, gs].unsqueeze(2).broadcast_to([128, ng, 16])
            k1 = wk13[:, gs, :]; k2 = wk23[:, gs, :]
            tt(k1, xr, bcg(pr), ALU.mult, R=Rl, W=[bwk])
            tt(k2, xi, bcg(pi), ALU.mult, R=Rl, W=[bwk])
            tt(outr, k1, k2, ALU.subtract, R=[bwk], W=Wl)
            tt(k1, xi, bcg(pr), ALU.mult, R=Rl, W=[bwk])
            tt(k2, xr, bcg(pi), ALU.mult, R=Rl, W=[bwk])
            if negi:
                stt(outi, k1, -1.0, k2, ALU.mult, ALU.subtract, R=[bwk], W=Wl)
            else:
                tt(outi, k1, k2, ALU.add, R=[bwk], W=Wl)

        cmul(Bbr3, Bbi3, crr, cii, Br3, Bi3, RB, [bBb])
        cp(Crb, Cr, R=[bCr], W=[bCb])
        ts(Cinb, Ci, -1.0, None, ALU.mult, None, R=[bCr], W=[bCb])
        cp(C1[:, 0, :], Pr3[:, 8, :], R=R_, W=[bS5c]); cp(C1[:, 1, :], Pr3[:, 8, :], R=R_, W=[bS5c])
        ts(C2[:, 0, :], Pi3[:, 8, :], -1.0, None, ALU.mult, None, R=R_, W=[bS5c])
        cp(C2[:, 1, :], Pi3[:, 8, :], R=R_, W=[bS5c])
        cp(CC3[:, 0:2, :], C1[:], R=[bS5c], W=[bS5c]); cp(CC3[:, 2:4, :], C2[:], R=[bS5c], W=[bS5c])
        memset(CC3[:, 4:6, :], 1.0, W=[bS5c])
        tt(w1, ar, ar, ALU.mult, R=R_, W=R_)
        tt(w2, ai, ai, ALU.mult, R=R_, W=R_)
        tt(w1, w1, w2, ALU.add, R=R_, W=R_)
        recip(w1, w1, R=R_, W=R_)
        tt(w2, ar, w1, ALU.mult, R=R_, W=R_)
        stt(w3, ai, -1.0, w1, ALU.mult, ALU.mult, R=R_, W=R_)
        for _ in range(2):
            tt(w1, w2, w2, ALU.mult, R=R_, W=R_)
            tt(th, w3, w3, ALU.mult, R=R_, W=R_)
            tt(w1, w1, th, ALU.subtract, R=R_, W=R_)
            stt(w3, w2, 2.0, w3, ALU.mult, ALU.mult, R=R_, W=R_)
            cp(w2, w1, R=R_, W=R_)
        cp(AI4[:, 0, :], w2, R=R_, W=[bS5c]); cp(AI4[:, 1, :], w3, R=R_, W=[bS5c])
        Yr3 = Yr.rearrange("p (g i) -> p g i", i=16); Yi3 = Yi.rearrange("p (g i) -> p g i", i=16)
        for k in range(8):
            cmul(Yr3, Yi3, Pr3[:, k, :], Pi3[:, k, :], Bbr3, Bbi3, [bpr, bBb], [bY])
            sBD, bsBD = stg[(2 * k) % 4]; sWI, bsWI = stg2[k % 2]
            for ft in range(16):
                h = ft // 8
                rows = slice(h * 64, (h + 1) * 64)
                cols = slice((ft % 8) * 128, (ft % 8 + 1) * 128)
                p, bp = nps()
                mm(p[:, :128], Yr[rows, cols], Crb[rows, cols], True, False, R=[bY, bCb], W=[bp])
                mm(p[:, :128], Yi[rows, cols], Cinb[rows, cols], False, True, R=[bY, bCb], W=[bp])
                tt(sBD[:, ft * 128:(ft + 1) * 128], p[:, :128], blk16[:], ALU.mult, R=[bp, bid], W=[bsBD])
                pt, bpt = npsT()
                tr(pt[:, 0:64], Yr[rows, cols], idb[rows, rows], R=[bY], W=[bpt])
                tr(pt[:, 64:128], Yi[rows, cols], idb[rows, rows], R=[bY], W=[bpt])
                cp(sWI[:, ft * 128:(ft + 1) * 128], pt[:, :128], R=[bpt], W=[bsWI], eng="act")
            s_ = 7 - k
            dma("sp", bd_d[:, :, k * 128:(k + 1) * 128].rearrange("f p c -> p f c"), sBD.rearrange("p (f c) -> p f c", f=16), R=[bsBD], W=[bscr])
            dma("sp", win_d[:, :, s_ * 128:(s_ + 1) * 128].rearrange("f p c -> p f c"), sWI.rearrange("p (f c) -> p f c", f=16), R=[bsWI], W=[bscr])
        Wh = LS[:, 0:8192]
        Wh5 = Wh.rearrange("p (g c t j) -> p g c t j", g=32, c=2, t=8)
        for gh in range(2):
            gs = slice(gh * 32, (gh + 1) * 32)
            for t_ in range(8):
                cmul(Wh5[:, :, 0, t_, :], Wh5[:, :, 1, t_, :], Pr3[:, t_ + 1, :], Pi3[:, t_ + 1, :], Cr3[:, gs, :], Ci3[:, gs, :],
                     [bpr, bCr], [bld, bldP], negi=True, gs=gs)
            dma("sp", wout2_d[:, gh * 8192:(gh + 1) * 8192], Wh, R=[bld, bldP], W=[bscr])
        memset(HH[0][:], 0.0, W=[bHH[0]])
        memset(HH[1][:], 0.0, W=[bHH[1]])

    A3 = bufA[:].rearrange("p (f n) -> p f n", f=16)
    B3 = bufB[:].rearrange("p (f n) -> p f n", f=16)
    scale_xa = 128.0 ** -0.5

    def project_fm(wsrc_cols, n_ftiles, N, sink):
        for f4 in range(0, n_ftiles, 4):
            nf = min(4, n_ftiles - f4)
            wt, bwt = wload(wsrc_cols(f4 * 128, nf * 128))
            for fo in range(nf):
                p, bp = nps()
                for k in range(8):
                    mm(p[:, :N], wt[:, k, fo * 128:(fo + 1) * 128], hT[:, k, :N], k == 0, k == 7, R=[bwt, bhT], W=[bp])
                sink(f4 + fo, p, bp)

    def xattn_prompt(l, N):
        ex = exP
        for h in range(4):
            e, be = ex[h % 2]
            e3 = e.rearrange("p (t n) -> p t n", t=2)
            for mt in range(2):
                p, bp = nps()
                mm(p[:, :N], kT_all[:, l, h, mt * 128:(mt + 1) * 128], qxa[:, h, :N], True, True, R=[bkT, bq], W=[bp])
                act(e3[:, mt, :N], p[:, :N], AF.Exp, R=[bp], W=[be], scale=scale_xa)
            pd, bpd = nps()
            for mt in range(2):
                mm(pd[:, :N], onesb[:], e3[:, mt, :N], mt == 0, mt == 1, R=[be, bid], W=[bpd])
            po, bpo = nps()
            for mt in range(2):
                mm(po[:, :N], v_all[:, l, mt, h * 128:(h + 1) * 128], e3[:, mt, :N], mt == 0, mt == 1, R=[be, bv], W=[bpo])
            rot()
            act(f1[:, :N], pd[:, :N], AF.Ln, R=[bpd], W=[bf1])
            act(f1[:, :N], f1[:, :N], AF.Exp, R=[bf1], W=[bf1], scale=-1.0)
            tt(qxa[:, h, :N], po[:, :N], f1[:, :N], ALU.mult, R=[bpo, bf1, bq], W=[bq])

    def xattn_sample(l):
        cvx = Carve()
        Kc = [cvx(f"Kc{i}", 1024, BF16) for i in range(4)]
        Vc = [cvx(f"Vc{i}", 1024, BF16) for i in range(4)]
        kTb = [cvx(f"kTb{i}", 1024, BF16) for i in range(2)]
        exs, bexs = cvx("exs", 512, BF16)
        ps_s, bps_s = nps()
        ps_o, bps_o = nps()
        vts = []
        for b in range(NSB):
            K_, bK = Kc[b % 4]; V_, bV = Vc[b % 4]; kt_, bkt = kTb[b % 2]
            K3 = K_.rearrange("p (t f) -> p t f", t=2); V3 = V_.rearrange("p (t f) -> p t f", t=2)
            dma("pool", K3, ck[l, b].rearrange("(t p) f -> p t f", p=128), W=[bK])
            dma("pool", V3, cv[l, b].rearrange("(t p) f -> p t f", p=128), W=[bV])
            pt, bpt = npsT()
            for h in range(4):
                for mt in range(2):
                    tr(pt[:, h * 256 + mt * 128: h * 256 + (mt + 1) * 128], K3[:, mt, h * 128:(h + 1) * 128], idb[:], R=[bK], W=[bpt])
            cp(kt_, pt[:], R=[bpt], W=[bkt], eng="act")
            for h in range(4):
                for mt in range(2):
                    c0 = mt * 256 + b * 16 + h * 4
                    mm(ps_s[:, c0:c0 + 4], kt_[:, h * 256 + mt * 128: h * 256 + (mt + 1) * 128], qxa[:, h, b * 4:(b + 1) * 4],
                       True, True, R=[bkt, bq], W=[bps_s])
            vts.append((V3, bV, b))
            for mt in range(2):
                c0 = mt * 256 + b * 16
                act(exs[:, c0:c0 + 16], ps_s[:, c0:c0 + 16], AF.Exp, R=[bps_s], W=[bexs], scale=scale_xa)
            for h in range(4):
                for mt in range(2):
                    c0 = mt * 256 + b * 16 + h * 4
                    mm(ps_o[:, b * 16 + h * 4: b * 16 + h * 4 + 4], V3[:, mt, h * 128:(h + 1) * 128], exs[:, c0:c0 + 4],
                       mt == 0, mt == 1, R=[bV, bexs], W=[bps_o])
        pd, bpd = nps()
        for mt in range(2):
            mm(pd[:, :256], onesb[:], exs[:, mt * 256:(mt + 1) * 256], mt == 0, mt == 1, R=[bexs, bid], W=[bpd])
        recip(f1[:, :256], pd[:, :256], R=[bpd], W=[bf1])
        tt(qxa[:, :, :64].rearrange("p h (b t) -> p b h t", b=NSB),
           ps_o[:, :256].rearrange("p (b h t) -> p b h t", b=NSB, h=4),
           f1[:, :256].rearrange("p (b h t) -> p b h t", b=NSB, h=4), ALU.mult, R=[bps_o, bf1, bq], W=[bq])

    def out_proj(l, N, rows_list, ycat):
        for dh in range(2):
            pl = [nps() for _ in rows_list]
            for u in range(3):
                kt = 8 if u < 2 else 4
                wt, bwt = wload(w_out[l, u * 1024:u * 1024 + kt * 128, dh * 512:(dh + 1) * 512])
                for ti, (tix, rows, c0) in enumerate(rows_list):
                    p, bp = pl[ti]
                    for k in range(kt):
                        ft = u * 8 + k
                        ya, by = ycat(ft)
                        mm(p[:rows, :], ya[:, c0:c0 + rows], wt[:, k, :], ft == 0, ft == 19, R=[bwt, by], W=[bp])
            for ti, (tix, rows, c0) in enumerate(rows_list):
                p, bp = pl[ti]
                xa_ = x_sb[:rows, tix, dh * 512:(dh + 1) * 512]
                tt(xa_, xa_, p[:rows, :], ALU.add, R=[bp, bx], W=[bx])

    def layer_sgu(l, j, N, tiles, sample):
        cvl = Carve()
        WtT, bWt = cvl("WtT", 1024, F32)
        biasB, bbias = cvl("biasB", 1024, F32)
        gsg, bgsg = cvl("gsg", E, F32)
        Wr = [cvl(f"Wr{i}", 1024, BF16) for i in range(2)]
        ssv, bssv = cvl("ssv", 32, F32)
        vraw, bvraw = cvl("vraw", E, F32)
        wq = w_in_a[j]
        dma("sp", gsg, sgu_norm_g[j:j + 1, :].partition_broadcast(128), W=[bgsg])
        WtT3 = WtT.rearrange("p (g t) -> p g t", g=8)
        bias3 = biasB.rearrange("p (g t) -> p g t", g=8)
        if not sample:
            CH = 128
            dma("sp", WtT3, sgu_wT[j].rearrange("g s t -> s g t"), W=[bWt])
            tri, btri = cvl("tri", 128, F32)
            dma("sp", tri, c_triu, W=[btri])
            tt(WtT3, WtT3, tri.unsqueeze(1).broadcast_to([128, 8, 128]), ALU.mult, R=[bWt, btri], W=[bWt])
            dma("sp", biasB, sgu_b[j:j + 1, :].partition_broadcast(128), W=[bbias])
        else:
            CH = 64
            w4, bw4 = cvl("w4", 32, F32)
            sel, bsel = cvl("sel", 64, F32)
            m64, bm64 = cvl("m64", 64, F32)
            dma("sp", sel[:4, :], c_sel, W=[bsel])
            dma("sp", m64[:64, :], c_m64, W=[bm64])
            dma("sp", w4[:4, :].rearrange("s (g t) -> s g t", g=8), sgu_wT[j, :, 0:4, 0:4].rearrange("g s t -> s g t"), W=[bw4])
            p, bp = nps()
            mm(p[:64, :32], sel[:4, :], w4[:4, :], True, True, R=[bsel, bw4], W=[bp])
            wrep, bwrep = cvl("wrep", 32, F32)
            cp(wrep[:64, :], p[:64, :32], R=[bp], W=[bwrep])
            Wb4 = WtT[:64, :512].rearrange("p (g b t) -> p g b t", g=8, b=16)
            tt(Wb4, wrep[:64, :].rearrange("p (g t) -> p g t", g=8).unsqueeze(2).broadcast_to([64, 8, 16, 4]),
               m64[:64, :].rearrange("p (b t) -> p b t", b=16).unsqueeze(1).broadcast_to([64, 8, 16, 4]), ALU.mult,
               R=[bwrep, bm64], W=[bWt])
            b4, bb4 = cvl("b4", 32, F32)
            dma("sp", b4.rearrange("p (g t) -> p g t", g=8),
                sgu_b[j].rearrange("(g t) -> g t", g=8)[:, 0:4].partition_broadcast(128), W=[bb4])
            tt(biasB[:, :512].rearrange("p (g b t) -> p g b t", g=8, b=16),
               b4.rearrange("p (g t) -> p g t", g=8).unsqueeze(2).broadcast_to([128, 8, 16, 4]),
               b4.rearrange("p (g t) -> p g t", g=8).unsqueeze(2).broadcast_to([128, 8, 16, 4]), ALU.max, R=[bb4], W=[bbias])
            WtT3 = WtT[:, :512].rearrange("p (g t) -> p g t", g=8)
            bias3 = biasB[:, :512].rearrange("p (g t) -> p g t", g=8)
        vtok = bufA[:, :4 * E].rearrange("p (t f) -> p t f", t=4)
        for cb in range(4):
            wt, bwt = wload(wq[:, E + cb * 512:E + (cb + 1) * 512])
            for ti, (tix, rows, c0) in enumerate(tiles):
                p, bp = nps()
                for k in range(8):
                    mm(p[:rows, :], hT[:, k, c0:c0 + rows], wt[:, k, :], k == 0, k == 7, R=[bwt, bhT], W=[bp])
                act(junk[:rows, :512], p[:rows, :], AF.Square, R=[bp], W=[bjunk, bssv], accum_out=ssv[:rows, ti * 4 + cb:ti * 4 + cb + 1])
                tt(vtok[:rows, ti, cb * 512:(cb + 1) * 512], p[:rows, :], gsg[:rows, cb * 512:(cb + 1) * 512], ALU.mult, R=[bp, bgsg], W=[bA])
                if sample:
                    tt(vraw[:rows, cb * 512:(cb + 1) * 512], p[:rows, :], gsg[:rows, cb * 512:(cb + 1) * 512], ALU.mult, R=[bp, bgsg], W=[bvraw])
        nt = len(tiles)
        S.op("dve", lambda e: e.tensor_reduce(out=ssv[:, 16:16 + nt], in_=ssv[:, :4 * nt].rearrange("p (t c) -> p t c", c=4),
                                              axis=mybir.AxisListType.X, op=ALU.add), R=[bssv], W=[bssv])
        rstd(ssv[:, 24:24 + nt], ssv[:, 16:16 + nt], E, R=[bssv], W=[bssv])
        if sample:
            ts(vraw[:64, :], vraw[:64, :], ssv[:64, 24:25], None, ALU.mult, None, R=[bvraw, bssv], W=[bvraw])
            final_ops.append(dma("sp", cv_s[j], vraw[:64, :], R=[bvraw]))
        wrs = []
        for ti, (tix, rows, c0) in enumerate(tiles):
            w_, bw_ = Wr[ti % 2] if nt > 2 else Wr[ti % 2]
            wrs.append((w_, bw_))
        if not sample:
            extra = [vraw[:, 0:512].bitcast(BF16), vraw[:, 512:1024].bitcast(BF16)]
            wrs = [Wr[0], Wr[1], (extra[0], bvraw), (extra[1], bvraw)]
        for ti, (tix, rows, c0) in enumerate(tiles):
            w_, bw_ = wrs[ti]
            ts(w_[:rows, :8 * CH], WtT[:rows, :8 * CH], ssv[:rows, 24 + ti:25 + ti], None, ALU.mult, None, R=[bWt, bssv], W=[bw_])
        for f4 in range(4):
            wu, bwu = wload(wq[:, f4 * 512:(f4 + 1) * 512])
            wz, bwz = wload(wq[:, 2 * E + f4 * 512:2 * E + (f4 + 1) * 512])
            for fo in range(4):
                f = f4 * 4 + fo
                g = f // 2
                pu, bpu = nps()
                for k in range(8):
                    mm(pu[:, :N], wu[:, k, fo * 128:(fo + 1) * 128], hT[:, k, :N], k == 0, k == 7, R=[bwu, bhT], W=[bpu])
                pz, bpz = nps()
                for k in range(8):
                    mm(pz[:, :N], wz[:, k, fo * 128:(fo + 1) * 128], hT[:, k, :N], k == 0, k == 7, R=[bwz, bhT], W=[bpz])
                pm, bpm = nps()
                for ti, (tix, rows, c0) in enumerate(tiles):
                    w_, bw_ = wrs[ti]
                    mm(pm[:, c0:c0 + rows], vtok[:rows, ti, f * 128:(f + 1) * 128], w_[:rows, g * CH:(g + 1) * CH], True, True, R=[bA, bw_], W=[bpm])
                rot()
                act(t1[:, :N], pz[:, :N], AF.Silu, R=[bpz], W=[bt1])
                tt(t2[:, :N], pu[:, :N], t1[:, :N], ALU.mult, R=[bpu, bt1], W=[bt2])
                tt(f1[:, :N].rearrange("p (c t) -> p c t", t=CH), pm[:, :N].rearrange("p (c t) -> p c t", t=CH),
                   bias3[:, g, :].unsqueeze(1).broadcast_to([128, N // CH, CH]), ALU.add, R=[bpm, bbias], W=[bf1])
                tt(B3[:, f, :N], t2[:, :N], f1[:, :N], ALU.mult, R=[bt2, bf1], W=[bB])
        project_fm(lambda c0, n: wq[:, 3 * E + c0:3 * E + c0 + n], 4, N,
                   lambda f, p, bp: cp(qxa[:, f, :N], p[:, :N], R=[bp], W=[bq], eng="act"))

    def layer_conv(l, N, tiles, sample, last_prompt):
        cvl = Carve()
        c32, bc32 = cvl("c32", 16 * 64, F32)
        c32_3 = c32.rearrange("p (f n) -> p f n", f=16)
        W_ = 6 if sample else 1
        if sample:
            cs3 = A3[:, :, :96].rearrange("p f (b w) -> p f b w", w=6)
            cdst = lambda f: cs3[:, f, :, 2:6]
            sct, bsct = cvl("sct", E, F32, parts=32)
            dma("sp", sct[:32, :], sconv, W=[bsct])
            for f4 in range(4):
                p, bp = nps()
                for fo in range(4):
                    f = f4 * 4 + fo
                    tr(p[:, fo * 32:(fo + 1) * 32], sct[:32, f * 128:(f + 1) * 128], idf[:32, :32], R=[bsct], W=[bp])
                cp(cs3[:, f4 * 4:(f4 + 1) * 4, :, 0:2], p[:, :128].rearrange("p (f b w) -> p f b w", f=4, w=2), R=[bp], W=[bA])
            view = lambda ap: ap.rearrange("p (b t) -> p b t", t=4)
            tap = lambda f, k: cs3[:, f, :, k:k + 4]
        else:
            cp(A3[:, :, 0:2], halo[:], R=[bhalo], W=[bA])
            cdst = lambda f: A3[:, f, 2:2 + N]
            view = lambda ap: ap
            tap = lambda f, k: A3[:, f, k:k + N]
        project_fm(lambda c0, n: w_in_b[:, c0:c0 + n], 16, N,
                   lambda f, p, bp: cp(B3[:, f, :N], p[:, :N], R=[bp], W=[bB], eng="act"))
        project_fm(lambda c0, n: w_in_b[:, E + c0:E + c0 + n], 16, N,
                   lambda f, p, bp: cp(cdst(f), view(p[:, :N]), R=[bp], W=[bA], eng="act"))

        def hv_sink(f, p, bp):
            if sample:
                tt(c32_3[:, f, :64].rearrange("p (b t) -> p b t", t=4), cdst(f), view(p[:, :N]), ALU.mult, R=[bp, bA], W=[bc32])
            elif last_prompt:
                tt(c32_3[:, f, 0:2], A3[:, f, N:N + 2], p[:, N - 2:N], ALU.mult, R=[bp, bA], W=[bc32])
            tt(cdst(f), cdst(f), view(p[:, :N]), ALU.mult, R=[bp, bA], W=[bA])
        project_fm(lambda c0, n: w_in_b[:, 2 * E + c0:2 * E + c0 + n], 16, N, hv_sink)
        if not sample:
            cp(halo[:], A3[:, :, N:N + 2], R=[bA], W=[bhalo])

        def z_sink(f, p, bp):
            rot()
            act(t1[:, :N], p[:, :N], AF.Silu, R=[bp], W=[bt1])
            act(view(f1[:, :N]), tap(f, 0), AF.Copy, R=[bA], W=[bf1], scale=wcv[:, f:f + 1])
            stt(view(f1[:, :N]), tap(f, 1), wcv[:, 16 + f:17 + f], view(f1[:, :N]), ALU.mult, ALU.add, R=[bA, bf1], W=[bf1])
            stt(view(f1[:, :N]), tap(f, 2), wcv[:, 32 + f:33 + f], view(f1[:, :N]), ALU.mult, ALU.add, R=[bA, bf1], W=[bf1])
            tt(t2[:, :N], B3[:, f, :N], t1[:, :N], ALU.mult, R=[bB, bt1], W=[bt2], eng="pool")
            tt(B3[:, f, :N], t2[:, :N], f1[:, :N], ALU.mult, R=[bt2, bf1, bB], W=[bB])
        project_fm(lambda c0, n: w_in_b[:, 3 * E + c0:3 * E + c0 + n], 16, N, z_sink)
        project_fm(lambda c0, n: w_in_b[:, 4 * E + c0:4 * E + c0 + n], 4, N,
                   lambda f, p, bp: cp(qxa[:, f, :N], p[:, :N], R=[bp], W=[bq], eng="act"))
        if sample or last_prompt:
            nco = 64 if sample else 2
            ost, bost = cvl("ost", E, F32, parts=64)
            for f4 in range(4):
                p, bp = nps()
                for fo in range(4):
                    f = f4 * 4 + fo
                    tr(p[:nco, fo * 128:(fo + 1) * 128], c32_3[:, f, :nco], idf[:], R=[bc32], W=[bp])
                cp(ost[:nco, f4 * 512:(f4 + 1) * 512], p[:nco, :], R=[bp], W=[bost])
            if sample:
                for b in range(NSB):
                    final_ops.append(dma("sp", conv_s[b], ost[b * 4 + 2:b * 4 + 4, :], R=[bost]))
            else:
                final_ops.append(dma("sp", conv_p, ost[:2, :], R=[bost]))

    def layer_s5(l, N, tiles, sample, last_prompt):
        cvl = Carve()
        NC_ = 16 if sample else N // 8
        NN = NC_ * 8
        ums = [cvl(f"um{i}", 8 * NN, BF16) for i in range(2)]
        umb = [[S.buf(f"um{i}_{g}", ls=True) for g in range(8)] for i in range(2)]
        SH, bSH = cvl("SH", 2 * 64 * NC_, BF16)
        NB5 = 3 if sample else 2
        BDb = [cvl(f"BDb{i}", 1024, BF16) for i in range(NB5)]
        Wib = [cvl(f"Wib{i}", 1024, BF16) for i in range(NB5)]
        NWO = 2 if sample else 1
        Wo2s = [cvl(f"Wo2{i}", 8 * 256, BF16) for i in range(NWO)]
        Yts = [cvl(f"Yt{i}", 1024, BF16, parts=64) for i in range(2)]
        xw1, bxw = cvl("xw1", 1024 if sample else 256, F32)
        xw2 = cvl("xw2", 1024, F32)[0] if sample else None
        SH4 = SH.rearrange("p (c g n) -> p c g n", c=2, g=64)
        if sample:
            HI4, bHI = SH4, bSH
        else:
            HI, bHI = cvl("Hhist", 2 * 64 * NC_, BF16)
            HI4 = HI.rearrange("p (c g n) -> p c g n", c=2, g=64)
        bAf = [S.buf(f"A_f{i}", ls=True) for i in range(16)]

        def zq_phase(part=None):
            for pt_ in (range(4) if part is None else ([part] if part < 4 else [])):
                project_fm(lambda c0, n, pt_=pt_: w_in_c[:, E + pt_ * 512 + c0:E + pt_ * 512 + c0 + n], 4, N,
                           lambda f, p, bp, pt_=pt_: act(B3[:, pt_ * 4 + f, :N], p[:, :N], AF.Silu, R=[bp], W=[bB]))
            if part is None or part == 4:
                project_fm(lambda c0, n: w_in_c[:, 2 * E + c0:2 * E + c0 + n], 4, N,
                           lambda f, p, bp: cp(qxa[:, f, :N], p[:, :N], R=[bp], W=[bq], eng="act"))
                if not sample:
                    xattn_prompt(l, N)

        if sample:
            memset(A3[:, :, :128], 0.0, W=bAf + [bA])
            usink = lambda f, p, bp: cp(A3[:, f, :128].rearrange("p (s b) -> p s b", s=8)[:, 4:8, :],
                                        p[:, :64].rearrange("p (b t) -> p t b", t=4), R=[bp], W=[bAf[f]], eng="act")
        else:
            usink = lambda f, p, bp: cp(A3[:, f, :N].rearrange("p (s n) -> p s n", s=8),
                                        p[:, :N].rearrange("p (n s) -> p s n", s=8), R=[bp], W=[bAf[f]], eng="act")
        project_fm(lambda c0, n: w_in_c[:, c0:c0 + n], 16, N, usink)
        mark(f's5 loop1')
        for ft in range(16):
            h = ft // 8
            wi, bwi = Wib[ft % NB5]
            dma("sp", wi, win_d[ft], R=[bscr], W=[bwi])
            wi4 = wi.rearrange("p (s c q) -> p s c q", s=8, c=2)
            um, _ = ums[ft % 2]
            bum8 = umb[ft % 2]
            um3 = um.rearrange("p (g n) -> p g n", g=8)
            for g8 in range(8):
                ts(um3[:, g8, :NN], A3[:, ft, :NN], mask8[:, g8:g8 + 1], None, ALU.mult, None, R=[bAf[ft]], W=[bum8[g8]])
            pS = [nps() for _ in range(2)]
            for c in range(2):
                p, bp = pS[c]
                for s in range(8):
                    mm(p[h * 64:(h + 1) * 64, :8 * NC_].rearrange("p (g n) -> p g n", g=8), wi4[:, s, c, :], um3[:, :, s * NC_:(s + 1) * NC_],
                       s == 0, s == 7, R=[bwi] + bum8, W=[bp], tile_position=(0, h * 64))
            gl = (ft % 8) * 8
            for c in range(2):
                p, bp = pS[c]
                cp(SH4[h * 64:(h + 1) * 64, c, gl:gl + 8, :NC_], p[h * 64:(h + 1) * 64, :8 * NC_].rearrange("p (g n) -> p g n", g=8),
                   R=[bp], W=[bSH], eng="act")
        last_l1 = S.q["pe"][-1]
        F3 = LS[:, 0:16 * NN].rearrange("p (f n) -> p f n", f=16)
        bFf = [S.buf(f"F_f{i}", ls=True) for i in range(16)]

        def fir_part(ft):
            bd, bbd = BDb[ft % NB5]
            dma("sp", bd, bd_d[ft], R=[bscr], W=[bbd])
            bd3 = bd.rearrange("p (t q) -> p t q", t=8)
            pf, bpf = nps()
            for tau in range(8):
                mm(pf[:, tau * NC_:NN], bd3[:, tau, :], A3[:, ft, 0:NN - tau * NC_], tau == 0, tau == 7, R=[bbd, bAf[ft]], W=[bpf])
            fdst = F3[:, ft, :NN]
            fsrc = pf[:, :NN]
            S.op("act", lambda e: e.activation(out=fdst, in_=fsrc, func=AF.Copy), R=[bpf], W=[bFf[ft]], extra=[last_l1])

        mark(f's5 recurrence')
        if not sample:
            a0 = s5state["cur"]
            cp(HH[a0][:, 4:6, :], SH4[:, :, :, 0], R=[bSH], W=[bHS[a0]], eng="act")
            for c in range(NC_):
                a = s5state["cur"]; b_ = 1 - a
                Ha, Hb = HH[a], HH[b_]
                cp(HI4[:, :, :, c], Ha[:, 0:2, :], R=[bHH[a]], W=[bHI], eng="act")
                if c + 1 < NC_:
                    cp(Hb[:, 4:6, :], SH4[:, :, :, c + 1], R=[bSH], W=[bHS[b_]], eng="act")
                tt(Pk[:], Ha[:], CC3[:], ALU.mult, R=[bHH[a], bHS[a], bS5c], W=[bPk])
                S.op("dve", lambda e, Hb=Hb: e.tensor_reduce(out=Hb[:, 0:2, :].rearrange("p c g -> p (c g)"),
                                                             in_=Pk[:].rearrange("p (k c) g -> p (c g) k", k=3),
                                                             axis=mybir.AxisListType.X, op=ALU.add), R=[bPk], W=[bHH[b_]])
                h2 = Hb[:, 0:2, :]
                rev = type(h2)(h2.tensor, h2.offset + 64, [[h2.ap[0][0], 128], [-64, 2], [1, 64]])
                cp(Hb[:, 2:4, :], rev, R=[bHH[b_]], W=[bHH[b_]])
                s5state["cur"] = b_
                if c >= 6 and (c - 6) % 3 == 0 and (c - 6) // 3 < 16:
                    fir_part((c - 6) // 3)
                if c >= 4 and (c - 4) % 8 == 0 and (c - 4) // 8 < 5:
                    zq_phase((c - 4) // 8)
            if last_prompt:
                Hf = HH[s5state["cur"]]; bHf = bHH[s5state["cur"]]
                for c, dst in enumerate((s5r_p, s5i_p)):
                    p, bp = nps()
                    tr(p[:64, :128], Hf[:, c, :], idf[:], R=[bHf], W=[bp])
                    cp(xw1[:64, c * 128:(c + 1) * 128], p[:64, :128], R=[bp], W=[bxw])
                    final_ops.append(dma("sp", dst.rearrange("(h g) q -> g h q", h=2),
                                         xw1[:64, c * 128:(c + 1) * 128].rearrange("g (h q) -> g h q", h=2), R=[bxw]))
        else:
            zq_phase()
            h0, bh0 = cvl("h0", 2 * 64 * 16, F32)
            h04 = h0.rearrange("p (c g b) -> p c g b", c=2, g=64)
            ldS, bldS = cvl("ldS", 16 * 128, F32, parts=64)
            for c, src in enumerate((s5r0, s5i0)):
                dma("sp", ldS.rearrange("g (b h q) -> g b h q", b=NSB, h=2), src.rearrange("b (h g) q -> g b h q", h=2), W=[bldS])
                for b4 in range(4):
                    p, bp = nps()
                    for bb in range(4):
                        b = b4 * 4 + bb
                        tr(p[:, bb * 64:(bb + 1) * 64], ldS[:64, b * 128:(b + 1) * 128], idf[:64, :64], R=[bldS], W=[bp])
                    cp(h04[:, c, :, b4 * 4:(b4 + 1) * 4], p[:, :256].rearrange("p (b g) -> p g b", b=4), R=[bp], W=[bh0])
            X4 = xw1.rearrange("p (c g b) -> p c g b", c=1, g=64); Y4 = xw2.rearrange("p (c g b) -> p c g b", c=1, g=64)
            bcb = lambda a: a.unsqueeze(3).broadcast_to([128, 2, 64, 16])
            bcb1 = lambda a: a.unsqueeze(2).broadcast_to([128, 64, 16])
            Hp, bHp = cvl("Hp", 2 * 64 * 16, F32)
            Hp4 = Hp.rearrange("p (c g b) -> p c g b", c=2, g=64)
            ar4 = AI4[:, 0, :]; ai4 = AI4[:, 1, :]
            tt(X4[:, 0], h04[:, 0], bcb1(ar4), ALU.mult, R=[bh0, bS5c], W=[bxw])
            tt(Y4[:, 0], h04[:, 1], bcb1(ai4), ALU.mult, R=[bh0, bS5c], W=[bxw])
            tt(Hp4[:, 0], X4[:, 0], Y4[:, 0], ALU.subtract, R=[bxw], W=[bHp])
            tt(X4[:, 0], h04[:, 1], bcb1(ar4), ALU.mult, R=[bh0, bS5c], W=[bxw])
            tt(Y4[:, 0], h04[:, 0], bcb1(ai4), ALU.mult, R=[bh0, bS5c], W=[bxw])
            tt(Hp4[:, 1], X4[:, 0], Y4[:, 0], ALU.add, R=[bxw], W=[bHp])
            a8r = C1[:, 0, :]; a8i = C2[:, 1, :]
            Hf, bHf_ = h0, bh0
            Hf4 = Hf.rearrange("p (c g b) -> p c g b", c=2, g=64)
            tt(X4[:, 0], Hp4[:, 0], bcb1(a8r), ALU.mult, R=[bHp, bS5c], W=[bxw])
            tt(Y4[:, 0], Hp4[:, 1], bcb1(a8i), ALU.mult, R=[bHp, bS5c], W=[bxw])
            tt(X4[:, 0], X4[:, 0], Y4[:, 0], ALU.subtract, R=[bxw], W=[bxw])
            tt(Hf4[:, 0], X4[:, 0], SH4[:, 0, :, :16], ALU.add, R=[bxw, bSH], W=[bHf_])
            tt(X4[:, 0], Hp4[:, 1], bcb1(a8r), ALU.mult, R=[bHp, bS5c], W=[bxw])
            tt(Y4[:, 0], Hp4[:, 0], bcb1(a8i), ALU.mult, R=[bHp, bS5c], W=[bxw])
            tt(X4[:, 0], X4[:, 0], Y4[:, 0], ALU.add, R=[bxw], W=[bxw])
            tt(Hf4[:, 1], X4[:, 0], SH4[:, 1, :, :16], ALU.add, R=[bxw, bSH], W=[bHf_])
            cp(SH4[:, :, :, :16], Hp4, R=[bHp, bSH], W=[bSH])
            for c, dst in enumerate((s5r_s, s5i_s)):
                for b4 in range(4):
                    p, bp = nps()
                    for bb in range(4):
                        b = b4 * 4 + bb
                        tr(p[:64, bb * 128:(bb + 1) * 128], Hf4[:, c, :, b], idf[:], R=[bHf_], W=[bp])
                    cp(ldS[:64, b4 * 512:(b4 + 1) * 512], p[:64, :], R=[bp], W=[bldS])
                final_ops.append(dma("sp", dst.rearrange("b (h g) q -> g b h q", h=2),
                                     ldS.rearrange("g (b h q) -> g b h q", b=NSB, h=2), R=[bldS]))
        mark(f's5 loop2')
        for ft in range(16):
            h = ft // 8
            gl = (ft % 8) * 8
            if sample:
                fir_part(ft)
            Wo2, bWo2 = Wo2s[ft % NWO]
            hr_ = slice(h * 64, (h + 1) * 64)
            dma("sp", Wo2[hr_, :], wout2_d[hr_, gl * 256:(gl + 8) * 256], R=[bscr], W=[bWo2])
            wo4 = Wo2.rearrange("p (g c f) -> p g c f", g=8, c=2)
            pst = [nps() for _ in range(2)]
            Yt, bYt = Yts[ft % 2]
            for g8 in range(8):
                for th in range(2):
                    p_, bp_ = pst[th]
                    for c in range(2):
                        mm(p_[:NC_, :].rearrange("n (t g j) -> n t g j", t=4, g=8)[:, :, g8, :], HI4[hr_, c, gl + g8, :NC_],
                           wo4[hr_, g8, c, th * 64:(th + 1) * 64].rearrange("p (t j) -> p t j", t=4), c == 0, c == 1,
                           R=[bWo2, bHI], W=[bp_], tile_position=(h * 64, 0))
            for th in range(2):
                p_, bp_ = pst[th]
                cp(Yt[:NC_, th * 512:(th + 1) * 512], p_[:NC_, :], R=[bp_], W=[bYt], eng="act" if th == 0 else "dve")
            py, bpy = nps()
            py_sm = py[:, :NN].rearrange("p (s n) -> p s n", s=8)
            u_sm = A3[:, ft, :NN].rearrange("p (s n) -> p s n", s=8)
            f_sm = F3[:, ft, :NN].rearrange("p (s n) -> p s n", s=8)
            for t_ in range(8):
                mm(py_sm[:, t_, :], Yt[:NC_, t_ * 128:(t_ + 1) * 128], idb[:NC_, :NC_], True, True, R=[bYt, bid], W=[bpy],
                   skip_group_check=True)
            rot()
            if sample:
                src_y = py_sm[:, 4:8, :]
                src_u = u_sm[:, 4:8, :]
                src_f = f_sm[:, 4:8, :]
                o1 = f1[:, :64].rearrange("p (b t) -> p t b", t=4)
            else:
                src_y = py_sm; src_u = u_sm; src_f = f_sm
                o1 = f1[:, :N].rearrange("p (n s) -> p s n", s=8)
            stt(o1, src_u, dcol[:, ft:ft + 1], src_y, ALU.mult, ALU.add, R=[bAf[ft], bpy], W=[bf1])
            tt(o1, o1, src_f, ALU.add, R=[bf1, bFf[ft]], W=[bf1])
            act(A3[:, ft, :N], f1[:, :N], AF.Gelu_apprx_tanh, R=[bf1], W=[bAf[ft]])
        mark(f's5 glu')
        for f4 in range(4):
            wa, bwa = wload(w_glu[0:1024, f4 * 512:(f4 + 1) * 512])
            wb, bwb = wload(w_glu[1024:2048, f4 * 512:(f4 + 1) * 512])
            for fo in range(4):
                f = f4 * 4 + fo
                p, bp = nps()
                for k in range(16):
                    wt_, bwt_ = (wa, bwa) if k < 8 else (wb, bwb)
                    mm(p[:, :N], wt_[:, k % 8, fo * 128:(fo + 1) * 128], A3[:, k, :N], k == 0, k == 15, R=[bwt_, bAf[k]], W=[bp])
                rot()
                act(t1[:, :N], p[:, :N], AF.Sigmoid, R=[bp], W=[bt1], bias=bgl[:, f:f + 1])
                tt(t2[:, :N], A3[:, f, :N], t1[:, :N], ALU.mult, R=[bAf[f], bt1], W=[bt2])
                tt(B3[:, f, :N], t2[:, :N], B3[:, f, :N], ALU.mult, R=[bt2, bB], W=[bB])

    if blocks is None:
        blocks = [("p", i) for i in range(4)] + [("s", 0)]
    for kind, bi in blocks:
        sample = kind == "s"
        if sample:
            N = NS
            tiles = [(0, 64, 0)]
            dma("sp", x_sb[:64, 0, :], xs, W=[bx])
        else:
            N = 512
            tiles = [(t, 128, t * 128) for t in range(4)]
            dma("sp", x_sb[:], xp[bi * 512:(bi + 1) * 512, :].rearrange("(t p) d -> p t d", p=128), W=[bx])
        last_prompt = (not sample) and bi == 3
        wstate["id"] = 0
        wstate["first"] = (kind, bi) == blocks[0]
        for l in range(n_layers):
            mark(f"{kind}{bi} L{l} start")
            dma("sp", gB[:], norm_g[l:l + 1, :].partition_broadcast(128), W=[bgB])
            barrier()
            norm_transpose_multi([(x_sb[:rows, tix, :], rows, hT[:, :, c0:c0 + rows]) for (tix, rows, c0) in tiles], bx, gB, bgB, bhT)
            kd = KINDS[l]
            if kd == 0:
                layer_sgu(l, KIDX[l], N, tiles, sample)
                ycat = lambda ft: ((B3[:, ft, :], bB) if ft < 16 else (qxa[:, ft - 16, :], bq))
            elif kd == 1:
                layer_conv(l, N, tiles, sample, last_prompt)
                ycat = lambda ft: ((B3[:, ft, :], bB) if ft < 16 else (qxa[:, ft - 16, :], bq))
            else:
                if do_s5:
                    layer_s5(l, N, tiles, sample, last_prompt)
                    ycat = lambda ft: ((B3[:, ft, :], bB) if ft < 16 else (qxa[:, ft - 16, :], bq))
                else:
                    continue
            mark(f"{kind}{bi} L{l} xattn")
            if sample:
                barrier()
                xattn_sample(l)
            elif kd != 2:
                xattn_prompt(l, N)
            mark(f"{kind}{bi} L{l} outproj")
            out_proj(l, N, tiles, ycat)
        dma("sp", gB[:], final_g.partition_broadcast(128), W=[bgB])
        for i, (tix, rows, c0) in enumerate(tiles):
            act(junk[:rows], x_sb[:rows, tix, :], AF.Square, R=[bx], W=[bjunk, bsm], accum_out=sm[:rows, i:i + 1])
        rstd(sm[:tiles[0][1], 8:8 + len(tiles)], sm[:tiles[0][1], 0:len(tiles)], D, R=[bsm], W=[bsm])
        ystg = bufB[:, :4 * 2 * D].bitcast(F32).rearrange("p (t d) -> p t d", t=4)
        for i, (tix, rows, c0) in enumerate(tiles):
            stt(ystg[:rows, tix, :], x_sb[:rows, tix, :], sm[:rows, 8 + i:9 + i], gB[:rows, :], ALU.mult, ALU.mult, R=[bx, bsm, bgB], W=[bB])
        if sample:
            final_ops.append(dma("sp", y_s, ystg[:64, 0, :], R=[bB]))
        else:
            final_ops.append(dma("sp", y_p[bi * 512:(bi + 1) * 512, :].rearrange("(t p) d -> p t d", p=128), ystg, R=[bB]))

    mark('end')
    S.emit(nc, es, final_ops)
    es.close()
    return nc


_CACHE = {}


def _consts():
    ident = np.eye(128, dtype=np.float32)
    triu = np.triu(np.ones((128, 128), dtype=np.float32))
    blk16 = np.kron(np.eye(8, dtype=np.float32), np.ones((16, 16), dtype=np.float32))
    m64 = np.kron(np.eye(16, dtype=np.float32), np.triu(np.ones((4, 4), dtype=np.float32)))
    sel = np.tile(np.eye(4, dtype=np.float32), (1, 16))
    mask8 = np.kron(np.eye(8, dtype=np.float32), np.ones((16, 1), dtype=np.float32))
    return dict(c_ident=ident, c_triu=triu, c_blk16=blk16, c_m64=m64, c_sel=sel, c_mask8=mask8)


def kernel(x_prompt, x_sample, mem_prompt, cache_mem_k, cache_mem_v, state_conv, state_s5_re, state_s5_im,
           norm_g, final_g, mem_norm_g, w_kv, w_out, w_in_a, sgu_norm_g, sgu_w, sgu_b, w_in_b, conv_w, w_in_c,
           s5_lam_re, s5_lam_im, s5_log_dt, s5_b_re, s5_b_im, s5_c_re, s5_c_im, s5_d, w_glu, b_glu):
    f = lambda a: np.ascontiguousarray(np.asarray(a, dtype=np.float32))
    if "nc" not in _CACHE:
        _CACHE["nc"] = build_program()
    nc = _CACHE["nc"]
    shared = dict(
        norm_g=f(norm_g), final_g=f(final_g).reshape(1, D), mem_norm_g=f(mem_norm_g), w_kv=f(w_kv), w_out=f(w_out),
        w_in_a=f(w_in_a), sgu_norm_g=f(sgu_norm_g), sgu_wT=f(np.transpose(np.asarray(sgu_w), (0, 1, 3, 2))),
        sgu_b=f(sgu_b).reshape(2, 1024), w_in_b=f(w_in_b)[0], conv_w=f(np.asarray(conv_w)[0].reshape(3, 16, 128).transpose(2, 0, 1).reshape(128, 48)),
        w_in_c=f(w_in_c)[0], lam_re=f(s5_lam_re)[0], lam_im=f(s5_lam_im)[0], log_dt=f(s5_log_dt).reshape(1, 128),
        b_re=f(s5_b_re)[0].reshape(128, 1024), b_im=f(s5_b_im)[0].reshape(128, 1024),
        c_re=f(s5_c_re)[0].reshape(128, 1024), c_im=f(s5_c_im)[0].reshape(128, 1024),
        s5_d=f(np.asarray(s5_d)[0].reshape(16, 128).T), w_glu=f(w_glu)[0], b_glu=f(np.asarray(b_glu)[0].reshape(16, 128).T),
    )
    shared.update(_consts())
    xp = f(x_prompt); xs = f(x_sample); mp = f(mem_prompt)
    ck = f(cache_mem_k).reshape(4, 128, MEM, 512); cvv = f(cache_mem_v).reshape(4, 128, MEM, 512)
    sc = f(state_conv)[0]; sr = f(state_s5_re)[0]; si = f(state_s5_im)[0]
    in_maps = []
    for c in range(8):
        b0, b1 = c * NSB, (c + 1) * NSB
        m = dict(shared)
        m.update(xp=xp[c], xs=xs[b0:b1].reshape(NS, D), mem=mp[c],
                 ck=np.ascontiguousarray(ck[:, b0:b1]), cv=np.ascontiguousarray(cvv[:, b0:b1]),
                 sconv=np.ascontiguousarray(sc[b0:b1].reshape(NSB * 2, E)),
                 s5r0=np.ascontiguousarray(sr[b0:b1]), s5i0=np.ascontiguousarray(si[b0:b1]))
        in_maps.append(m)
    if _CACHE.get("dbg_cores"):
        cores = _CACHE["dbg_cores"]
        res = run_bass_kernel_spmd(nc, [in_maps[c] for c in cores], core_ids=list(range(len(cores))))
        return res.results
    res = run_bass_kernel_spmd(nc, in_maps, core_ids=list(range(8)))
    R = res.results
    cat = lambda k: np.stack([np.asarray(r[k], dtype=np.float32) for r in R])
    y_prompt = cat("y_p")
    y_sample = cat("y_s").reshape(128, 4, D)
    mk = cat("mk_o").transpose(1, 0, 2, 3).reshape(4, 8, MEM, 4, 128)
    mv = cat("mv_o").transpose(1, 0, 2, 3).reshape(4, 8, MEM, 4, 128)
    conv_p = cat("conv_p")[None]
    conv_s = cat("conv_s").reshape(1, 128, 2, E)
    s5rp = cat("s5r_p")[None]; s5ip = cat("s5i_p")[None]
    s5rs = cat("s5r_s").reshape(1, 128, 128, 64); s5is = cat("s5i_s").reshape(1, 128, 128, 64)
    cvs = cat("cv_s").transpose(1, 0, 2, 3).reshape(2, 128, 4, E)
    return (y_prompt, y_sample, np.ascontiguousarray(mk), np.ascontiguousarray(mv), conv_p, conv_s,
            s5rp, s5ip, s5rs, s5is, np.ascontiguousarray(cvs))
```

```python
import contextlib
import math
import numpy as np
import concourse.bass as bass
import concourse.mybir as mybir
from concourse.bass_utils import run_bass_kernel_spmd

F32 = mybir.dt.float32
BF16 = mybir.dt.bfloat16
I32 = mybir.dt.int32
AF = mybir.ActivationFunctionType
ALU = mybir.AluOpType

D = 1024
E = 2048
SEQ = 2048
NSB = 16
NS = 64
MEM = 256
EPS = 1e-6
KINDS = (0, 1, 2, 0)
KIDX = (0, 0, 0, 1)


class Buf:
    __slots__ = ("name", "w", "r", "excl")

    def __init__(self, name, excl=False):
        self.name = name
        self.w = None
        self.r = []
        self.excl = excl


class Op:
    __slots__ = ("eng", "fn", "deps", "isdma", "sig", "sem", "val", "nobar")


class Sched:
    ENG = ("pe", "act", "dve", "pool", "sp")
    NDMASEM = 20

    def __init__(self):
        self.q = {e: [] for e in self.ENG}
        self.dma_last = {}
        self.dma_cnt = {}
        self.dma_rr = {e: 0 for e in self.ENG}
        self.last_barrier = None
        self.lsbufs = []

    def buf(self, name, ls=False, excl=False):
        b = Buf(name, excl)
        if ls:
            b.w = self.last_barrier
            self.lsbufs.append(b)
        return b

    def op(self, eng, fn, R=(), W=(), dma=False, extra=()):
        o = Op()
        o.eng, o.fn, o.isdma, o.sig, o.deps = eng, fn, dma, False, []
        o.nobar = bool(dma) and len(W) + len(R) > 0 and all(getattr(b, "name", "").startswith(("wu", "wsc")) for b in list(W) + list(R))

        def add(d):
            if d is None or d is o or d in o.deps:
                return
            if (not d.isdma) and (not dma) and d.eng == "pe" and eng == "pe":
                return
            o.deps.append(d)

        for d in extra:
            add(d)
        for b in R:
            add(b.w)
            if b.excl:
                for r in b.r:
                    if r.eng != eng:
                        add(r)
        for b in W:
            add(b.w)
            for r in b.r:
                add(r)
        if dma:
            slot = self.dma_rr[eng] % self.NDMASEM
            self.dma_rr[eng] += 1
            key = (eng, slot)
            add(self.dma_last.get(key))
            self.dma_last[key] = o
            self.dma_cnt[key] = self.dma_cnt.get(key, 0) + 1
            o.sem = key
            o.val = 16 * self.dma_cnt[key]
        for b in R:
            b.r.append(o)
        for b in W:
            b.w = o
            b.r = []
        self.q[eng].append(o)
        return o

    def barrier(self, fn):
        extra = []
        for e in self.ENG:
            for o in reversed(self.q[e]):
                if not o.isdma:
                    extra.append(o)
                    break
        extra += [d for d in self.dma_last.values() if not d.nobar]
        x = self.op("dve", fn, extra=extra)
        self.last_barrier = x
        self.lsbufs = []
        return x

    def emit(self, nc, es, final_ops):
        for e in self.ENG:
            for o in self.q[e]:
                for d in o.deps:
                    d.sig = True
        for o in final_ops:
            o.sig = True
        csem = {e: es.enter_context(nc.semaphore(f"c_{e}")) for e in self.ENG}
        dsem = {k: es.enter_context(nc.semaphore(f"d_{k[0]}_{k[1]}")) for k in self.dma_cnt}
        for e in self.ENG:
            c = 0
            for o in self.q[e]:
                if o.isdma:
                    o.sem = dsem[o.sem]
                else:
                    if o.sig:
                        c += 1
                        o.val = c
                    o.sem = csem[e]
        sched = self

        def run(e, eng):
            waited = {}

            def w(d):
                k = id(d.sem)
                if waited.get(k, 0) < d.val:
                    eng.wait_ge(d.sem, d.val)
                    waited[k] = d.val

            for o in sched.q[e]:
                for d in o.deps:
                    w(d)
                inst = o.fn(eng)
                if o.isdma:
                    inst.then_inc(o.sem, 16)
                elif o.sig:
                    inst.then_inc(o.sem, 1)
            if e == "sp":
                for d in final_ops:
                    w(d)

        with nc.Block() as block:
            @block.tensor
            def _(eng):
                run("pe", eng)

            @block.scalar
            def _(eng):
                run("act", eng)

            @block.vector
            def _(eng):
                run("dve", eng)

            @block.gpsimd
            def _(eng):
                run("pool", eng)

            @block.sync
            def _(eng):
                run("sp", eng)


def build_program(n_layers=4, do_s5=True, blocks=None, s5prep=True):
    nc = bass.Bass("TRN2", target_bir_lowering=False)
    S = Sched()
    es = contextlib.ExitStack()

    def din(name, shape, dt=F32):
        return nc.dram_tensor(name, list(shape), dt, kind="ExternalInput").ap()

    def dout(name, shape, dt=F32):
        return nc.dram_tensor(name, list(shape), dt, kind="ExternalOutput").ap()

    xp = din("xp", [SEQ, D]); xs = din("xs", [NS, D]); mem = din("mem", [MEM, D])
    ck = din("ck", [4, NSB, MEM, 512]); cv = din("cv", [4, NSB, MEM, 512])
    sconv = din("sconv", [NSB * 2, E]); s5r0 = din("s5r0", [NSB, 128, 64]); s5i0 = din("s5i0", [NSB, 128, 64])
    norm_g = din("norm_g", [4, D]); final_g = din("final_g", [1, D]); mem_norm_g = din("mem_norm_g", [4, D])
    w_kv = din("w_kv", [4, D, D]); w_out = din("w_out", [4, E + 512, D])
    w_in_a = din("w_in_a", [2, D, 3 * E + 512]); sgu_norm_g = din("sgu_norm_g", [2, E])
    sgu_wT = din("sgu_wT", [2, 8, 128, 128]); sgu_b = din("sgu_b", [2, 8 * 128])
    w_in_b = din("w_in_b", [D, 4 * E + 512]); conv_w = din("conv_w", [128, 48])
    w_in_c = din("w_in_c", [D, 2 * E + 512])
    lam_re = din("lam_re", [128, 64]); lam_im = din("lam_im", [128, 64]); log_dt = din("log_dt", [1, 128])
    b_re = din("b_re", [128, 1024]); b_im = din("b_im", [128, 1024])
    c_re = din("c_re", [128, 1024]); c_im = din("c_im", [128, 1024])
    s5_d = din("s5_d", [128, 16]); w_glu = din("w_glu", [E, E]); b_glu = din("b_glu", [128, 16])
    c_ident = din("c_ident", [128, 128]); c_triu = din("c_triu", [128, 128]); c_blk16 = din("c_blk16", [128, 128])
    c_m64 = din("c_m64", [64, 64]); c_sel = din("c_sel", [4, 64]); c_mask8 = din("c_mask8", [128, 8])
    y_p = dout("y_p", [SEQ, D]); y_s = dout("y_s", [NS, D])
    mk_o = dout("mk_o", [4, MEM, 512]); mv_o = dout("mv_o", [4, MEM, 512])
    conv_p = dout("conv_p", [2, E]); conv_s = dout("conv_s", [NSB, 2, E])
    s5r_p = dout("s5r_p", [128, 64]); s5i_p = dout("s5i_p", [128, 64])
    s5r_s = dout("s5r_s", [NSB, 128, 64]); s5i_s = dout("s5i_s", [NSB, 128, 64])
    cv_s = dout("cv_s", [2, NS, E])
    bd_d = nc.dram_tensor("bd_d", [16, 128, 1024], BF16, kind="Internal").ap()
    win_d = nc.dram_tensor("win_d", [16, 128, 1024], BF16, kind="Internal").ap()
    wout2_d = nc.dram_tensor("wout2_d", [128, 64 * 256], BF16, kind="Internal").ap()

    final_ops = []
    marks = []
    _CACHE['marks'] = marks

    def mark(lbl):
        marks.append((lbl, len(S.q['pe'])))
    bscr = S.buf("dram_scratch")

    def sb(name, shape, dt):
        return es.enter_context(nc.sbuf_tensor(name, list(shape), dt))

    x_sb = sb("x_sb", [128, 4, D], F32); bx = S.buf("x")
    hT = sb("hT", [128, 8, 512], BF16); bhT = S.buf("hT")
    bhn4 = [S.buf(f"hn{i}") for i in range(4)]
    junk = sb("junk", [128, D], BF16); bjunk = S.buf("junk")
    NW = 4
    wun = [sb(f"wu{i}", [128, 8, 512], BF16) for i in range(NW)]; bw = [S.buf(f"wu{i}") for i in range(NW)]
    FA = 520
    bufA = sb("bufA", [128, 16 * FA], BF16); bA = S.buf("A")
    hn4 = bufA[:, :4 * D].rearrange("p (t d) -> p t d", t=4)
    bufB = sb("bufB", [128, 16 * FA], BF16); bB = S.buf("B")
    qxa = sb("qxa", [128, 4, 512], BF16); bq = S.buf("qxa")
    exP = [(sb(f"exP{i}", [128, 1024], BF16), S.buf(f"exP{i}")) for i in range(2)]
    NROT = 3
    T1 = [(sb(f"t1_{i}", [128, 512], BF16), S.buf(f"t1_{i}")) for i in range(NROT)]
    T2 = [(sb(f"t2_{i}", [128, 512], BF16), S.buf(f"t2_{i}")) for i in range(NROT)]
    F1 = [(sb(f"f1_{i}", [128, 512], F32), S.buf(f"f1_{i}")) for i in range(NROT)]
    (t1, bt1), (t2, bt2), (f1, bf1) = T1[0], T2[0], F1[0]
    roti = [0]

    def rot():
        nonlocal t1, bt1, t2, bt2, f1, bf1
        roti[0] = (roti[0] + 1) % NROT
        (t1, bt1), (t2, bt2), (f1, bf1) = T1[roti[0]], T2[roti[0]], F1[roti[0]]
    kT_all = sb("kT_all", [128, 4, 4, 256], BF16); bkT = S.buf("kT")
    v_all = sb("v_all", [128, 4, 2, 512], BF16); bv = S.buf("v")
    gB = sb("gB", [128, D], F32); bgB = S.buf("gB")
    idf = sb("idf", [128, 128], F32); idb = sb("idb", [128, 128], BF16); bid = S.buf("id")
    onesb = sb("onesb", [128, 128], BF16)
    sm = sb("sm", [128, 64], F32); bsm = S.buf("sm")
    mask8 = sb("mask8", [128, 8], F32)
    neghalf = sb("neghalf", [128, 16], F32)
    blk16 = sb("blk16", [128, 128], F32)
    halo = sb("halo", [128, 16, 2], BF16); bhalo = S.buf("halo")
    C1 = sb("C1", [128, 2, 64], F32); C2 = sb("C2", [128, 2, 64], F32); AI4 = sb("AI4", [128, 2, 64], F32)
    bS5c = S.buf("s5c")
    HH = [sb(f"HH{i}", [128, 6, 64], F32) for i in range(2)]
    bHH = [S.buf(f"HHh{i}") for i in range(2)]; bHS = [S.buf(f"HHs{i}") for i in range(2)]
    CC3 = sb("CC3", [128, 6, 64], F32)
    Pk = sb("Pk", [128, 6, 64], F32); bPk = S.buf("Pk")
    dcol = sb("dcol", [128, 16], F32); bgl = sb("bgl", [128, 16], F32); wcv = sb("wcv", [128, 48], F32)
    LSN = 33792
    LS = sb("LS", [128, LSN], BF16)
    NPS = 7
    ps = [es.enter_context(nc.psum_tensor(f"ps{i}", [128, 512], F32)) for i in range(NPS)]
    bps = [S.buf(f"ps{i}", excl=True) for i in range(NPS)]
    psT = [es.enter_context(nc.psum_tensor(f"psT{i}", [128, 1024], BF16)) for i in range(1)]
    bpsT = [S.buf(f"psT{i}", excl=True) for i in range(1)]
    rr = {"ps": 0, "psT": 0, "w": 0}

    def nps():
        i = rr["ps"] % NPS
        rr["ps"] += 1
        return ps[i], bps[i]

    def npsT():
        i = rr["psT"] % 1
        rr["psT"] += 1
        return psT[i], bpsT[i]

    class Carve:
        def __init__(self, extra=False):
            self.regions = [(LS, LSN)] + ([(bufA, 16 * FA), (bufB, 16 * FA)] if extra else [])
            self.ri = 0
            self.off = 0

        def __call__(self, name, n, dt, parts=128, buf=None):
            k = n if dt == BF16 else 2 * n
            if self.off % 2:
                self.off += 1
            if self.off + k > self.regions[self.ri][1]:
                self.ri += 1
                self.off = 0
                assert self.ri < len(self.regions), (name, "scratch overflow")
            t_ = self.regions[self.ri][0]
            a = t_[:parts, self.off:self.off + k]
            self.off += k
            if dt != BF16:
                a = a.bitcast(dt)
            return a, (buf if buf is not None else S.buf(name, ls=True))

    def barrier():
        x = S.barrier(lambda e: e.memset(sm[:, 63:64], 0.0))
        for b_ in (bA, bB):
            b_.w = x
            b_.r = []

    def dma(eng, out, in_, R=(), W=()):
        return S.op(eng, lambda e: e.dma_start(out=out, in_=in_), R=R, W=W, dma=True)

    def mm(out, lhsT, rhs, start, stop, R, W, **kw):
        return S.op("pe", lambda e: e.matmul(out, lhsT=lhsT, rhs=rhs, start=start, stop=stop, **kw), R=R, W=W)

    def tr(out, in_, ident, R, W):
        return S.op("pe", lambda e: e.transpose(out=out, in_=in_, identity=ident), R=list(R) + [bid], W=W)

    def act(out, in_, func, R, W, eng="act", **kw):
        return S.op(eng, lambda e: e.activation(out=out, in_=in_, func=func, **kw), R=R, W=W)

    def tt(out, in0, in1, op, R, W, eng="dve"):
        return S.op(eng, lambda e: e.tensor_tensor(out=out, in0=in0, in1=in1, op=op), R=R, W=W)

    def ts(out, in0, s1, s2, op0, op1, R, W, eng="dve"):
        if op1 is None:
            return S.op(eng, lambda e: e.tensor_scalar(out=out, in0=in0, scalar1=s1, scalar2=None, op0=op0), R=R, W=W)
        return S.op(eng, lambda e: e.tensor_scalar(out=out, in0=in0, scalar1=s1, scalar2=s2, op0=op0, op1=op1), R=R, W=W)

    def stt(out, in0, scalar, in1, op0, op1, R, W, eng="dve"):
        return S.op(eng, lambda e: e.scalar_tensor_tensor(out=out, in0=in0, scalar=scalar, in1=in1, op0=op0, op1=op1), R=R, W=W)

    def cp(out, in_, R, W, eng="dve"):
        if eng == "act":
            return S.op("act", lambda e: e.activation(out=out, in_=in_, func=AF.Copy), R=R, W=W)
        return S.op(eng, lambda e: e.tensor_copy(out=out, in_=in_), R=R, W=W)

    def memset(ap, val, W, eng="dve"):
        return S.op(eng, lambda e: e.memset(ap, val), W=W)

    def recip(out, in_, R, W):
        return S.op("dve", lambda e: e.reciprocal(out=out, in_=in_), R=R, W=W)

    NUNITS = 96
    wsc = nc.dram_tensor("wsc", [NUNITS, 128, 8 * 512], BF16, kind="Internal").ap()
    wstate = {"id": None, "first": True}
    wkeys = {}
    bwsc = [S.buf(f"wsc{i}") for i in range(NUNITS)]

    def wload(src):
        i = rr["w"] % NW
        rr["w"] += 1
        kt = src.shape[0] // 128
        cols = src.shape[1]
        if wstate["id"] is None:
            dma("pool", wun[i][:, :kt, :cols], src.rearrange("(k p) c -> p k c", p=128), W=[bw[i]])
            return wun[i], bw[i]
        key = (str(src.tensor.name), int(src.offset), tuple(src.shape))
        if wstate["first"]:
            assert key not in wkeys
            wkeys[key] = len(wkeys)
        uid = wkeys[key]
        assert uid < NUNITS
        flat = wun[i][:].rearrange("p k c -> p (k c)")
        if wstate["first"]:
            dma("pool", wun[i][:, :kt, :cols], src.rearrange("(k p) c -> p k c", p=128), W=[bw[i]])
            if cols == 512:
                dma("sp", wsc[uid, :, :kt * 512], flat[:, :kt * 512], R=[bw[i]], W=[bwsc[uid]])
            else:
                dma("sp", wsc[uid].rearrange("p (k c) -> p k c", k=8)[:, :kt, :cols], wun[i][:, :kt, :cols], R=[bw[i]], W=[bwsc[uid]])
        else:
            if cols == 512:
                dma("sp", flat[:, :kt * 512], wsc[uid, :, :kt * 512], R=[bwsc[uid]], W=[bw[i]])
            else:
                dma("sp", wun[i][:, :kt, :cols], wsc[uid].rearrange("p (k c) -> p k c", k=8)[:, :kt, :cols], R=[bwsc[uid]], W=[bw[i]])
        return wun[i], bw[i]

    dma("sp", idf[:], c_ident, W=[bid])
    cp(idb[:], idf[:], R=[bid], W=[bid])
    memset(onesb[:], 1.0, W=[bid])
    memset(neghalf[:], -0.5, W=[bid])
    dma("sp", mask8[:], c_mask8, W=[bid])
    dma("sp", blk16[:], c_blk16, W=[bid])
    dma("sp", dcol[:], s5_d, W=[bid])
    dma("sp", bgl[:], b_glu, W=[bid])
    dma("sp", wcv[:], conv_w, W=[bid])
    memset(halo[:], 0.0, W=[bhalo])

    def rstd(out_ap, ss_ap, n, R, W):
        ts(out_ap, ss_ap, 1.0 / n, EPS, ALU.mult, ALU.add, R=R, W=W)
        shp = list(out_ap.shape)
        tt(out_ap, out_ap, neghalf[:shp[0], :shp[1]], ALU.pow, R=W, W=W, eng="pool")

    def norm_transpose_multi(items, bxsrc, g_full, bg, bdst):
        n = len(items)
        for i, (x_ap, rows, dst) in enumerate(items):
            act(junk[:rows], x_ap, AF.Square, R=[bxsrc], W=[bjunk, bsm], accum_out=sm[:rows, i:i + 1])
        rmax = max(r for _, r, _ in items)
        rstd(sm[:rmax, 8:8 + n], sm[:rmax, 0:n], D, R=[bsm], W=[bsm])
        for i, (x_ap, rows, dst) in enumerate(items):
            stt(hn4[:rows, i, :], x_ap, sm[:rows, 8 + i:9 + i], g_full[:rows, :], ALU.mult, ALU.mult, R=[bxsrc, bsm, bg], W=[bhn4[i]] + ([bA] if i == 0 else []))
        for i, (x_ap, rows, dst) in enumerate(items):
            pt, bpt = npsT()
            for k in range(8):
                tr(pt[:, k * 128:k * 128 + rows], hn4[:rows, i, k * 128:(k + 1) * 128], idb[:rows, :rows], R=[bhn4[i], bA], W=[bpt])
            cp(dst, pt[:].rearrange("p (k t) -> p k t", k=8)[:, :, :rows], R=[bpt], W=[bdst], eng="act")

    barrier()
    cv_ = Carve()
    mem_sb, bmem = cv_("mem_sb", 2 * D, F32)
    memT, bmemT = cv_("memT", 8 * 256, BF16)
    kvst = [cv_(f"kvst{i}", 512, F32) for i in range(2)]
    mem3 = mem_sb.rearrange("p (t d) -> p t d", t=2)
    memT3 = memT.rearrange("p (k m) -> p k m", k=8)
    dma("sp", mem3, mem.rearrange("(t p) d -> p t d", p=128), W=[bmem])
    gm, bgm = cv_("gm", D, F32)
    kvi = 0
    import os
    for l in range(0 if os.environ.get('DBG_SKIP_MEMKV') else max(n_layers, int(os.environ.get('DBG_MEMKV', '0')))):
        dma("sp", gm, mem_norm_g[l:l + 1, :].partition_broadcast(128), W=[bgm])
        STG = int(os.environ.get('DBG_STAGE', '9'))
        norm_transpose_multi([(mem3[:, t, :], 128, memT3[:, :, t * 128:(t + 1) * 128]) for t in range(2)], bmem, gm, bgm, bmemT)
        for half in range(2):
            if STG < 2:
                continue
            wt, bwt = wload(w_kv[l, :, half * 512:(half + 1) * 512])
            if STG < 3:
                continue
            if half == 0:
                for h in range(4):
                    p, bp = nps()
                    for k in range(8):
                        mm(p[:, :256], wt[:, k, h * 128:(h + 1) * 128], memT3[:, k, :], k == 0, k == 7, R=[bwt, bmemT], W=[bp])
                    cp(kT_all[:, l, h, :], p[:, :256], R=[bp], W=[bkT], eng="act")
            if STG < 4:
                continue
            for t in range(2):
                p, bp = nps()
                for k in range(8):
                    mm(p[:], memT3[:, k, t * 128:(t + 1) * 128], wt[:, k, :], k == 0, k == 7, R=[bwt, bmemT], W=[bp])
                st, bst = kvst[kvi % 2]
                kvi += 1
                cp(st, p[:], R=[bp], W=[bst])
                if half == 1:
                    cp(v_all[:, l, t, :], p[:], R=[bp], W=[bv], eng="act")
                dst = (mk_o if half == 0 else mv_o)[l, t * 128:(t + 1) * 128, :]
                if STG >= 5:
                    final_ops.append(dma("sp", dst, st, R=[bst]))

    s5state = {"cur": 0}
    if do_s5 and s5prep and n_layers > 2:
        barrier()
        cv_ = Carve(extra=True)
        ldT, bld = cv_("ldT", 2 * 1024, F32, parts=64)
        ldP, bldP = cv_("ldP", 2 * 1024, F32, parts=64)
        lr, bpr = cv_("lr", 64, F32); li, _ = cv_("li", 64, F32); dtb, _ = cv_("dtb", 64, F32)
        th, _ = cv_("th", 64, F32); kf, _ = cv_("kf", 64, F32); ki, _ = cv_("ki", 64, I32)
        sn, _ = cv_("sn", 64, F32); cs, _ = cv_("cs", 64, F32); mag, _ = cv_("mag", 64, F32)
        w1, _ = cv_("w1", 64, F32); w2, _ = cv_("w2", 64, F32); w3, _ = cv_("w3", 64, F32)
        crr, _ = cv_("crr", 64, F32); cii, _ = cv_("cii", 64, F32)
        Pr, _ = cv_("Pr", 9 * 64, F32); Pi, _ = cv_("Pi", 9 * 64, F32)
        Br, bBr = cv_("Br", 1024, F32); Bi, _ = cv_("Bi", 1024, F32)
        Cr, bCr = cv_("Cr", 1024, F32); Ci, _ = cv_("Ci", 1024, F32)
        Bbr, bBb = cv_("Bbr", 1024, F32); Bbi, _ = cv_("Bbi", 1024, F32)
        Crb, bCb = cv_("Crb", 1024, BF16); Cinb, _ = cv_("Cinb", 1024, BF16)
        Yr, bY = cv_("Yr", 1024, BF16); Yi, _ = cv_("Yi", 1024, BF16)
        wk1, bwk = cv_("wk1", 1024, F32); wk2, _ = cv_("wk2", 1024, F32)
        stg = [cv_(f"stg{i}", 2048, BF16) for i in range(4)]
        stg2 = [cv_(f"stgw{i}", 2048, BF16) for i in range(2)]
        Pr3 = Pr.rearrange("p (k g) -> p k g", k=9); Pi3 = Pi.rearrange("p (k g) -> p k g", k=9)
        R_ = [bpr]

        def loadT(src, dst_of_col, ncol, stride, bdst):
            ld3 = ldT.rearrange("g (h f) -> g h f", h=2)
            nf = src.shape[1]
            dma("sp", ld3[:, :, :nf], src.rearrange("(h g) f -> g h f", h=2), W=[bld])
            ldP4 = ldP[:, :2 * nf].rearrange("g (c h q) -> g c h q", h=2, q=64)
            if stride == 1:
                src4 = ld3[:, :, :nf].rearrange("g h (c q) -> g c h q", q=64)
            else:
                src4 = ld3[:, :, :nf].rearrange("g h (q c) -> g c h q", c=stride)
            cp(ldP4, src4, R=[bld], W=[bldP])
            for c in range(ncol):
                p, bp = nps()
                tr(p[:, :64], ldP[:, c * 128:(c + 1) * 128], idf[:64, :64], R=[bldP], W=[bp])
                cp(dst_of_col(c), p[:, :64], R=[bp], W=[bdst], eng="act")

        loadT(lam_re, lambda c: lr, 1, 1, bpr)
        loadT(lam_im, lambda c: li, 1, 1, bpr)
        Br3 = Br.rearrange("p (g i) -> p g i", i=16); Bi3 = Bi.rearrange("p (g i) -> p g i", i=16)
        Cr3 = Cr.rearrange("p (g j) -> p g j", j=16); Ci3 = Ci.rearrange("p (g j) -> p g j", j=16)
        loadT(b_re, lambda c: Br3[:, :, c], 16, 16, bBr)
        loadT(b_im, lambda c: Bi3[:, :, c], 16, 16, bBr)
        loadT(c_re, lambda c: Cr3[:, :, c], 16, 1, bCr)
        loadT(c_im, lambda c: Ci3[:, :, c], 16, 1, bCr)
        for h in range(2):
            dma("sp", dtb[h * 64:(h + 1) * 64, :], log_dt[:, h * 64:(h + 1) * 64].partition_broadcast(64), W=[bpr])
        act(dtb, dtb, AF.Exp, R=R_, W=R_)
        tt(th, li, dtb, ALU.mult, R=R_, W=R_)
        ts(kf, th, 1.0 / (2 * math.pi), None, ALU.mult, None, R=R_, W=R_)
        cp(ki, kf, R=R_, W=R_)
        cp(kf, ki, R=R_, W=R_)
        stt(th, kf, -2 * math.pi, th, ALU.mult, ALU.add, R=R_, W=R_)
        ts(w1, th, math.pi, -2 * math.pi, ALU.is_gt, ALU.mult, R=R_, W=R_)
        tt(th, th, w1, ALU.add, R=R_, W=R_)
        ts(w1, th, -math.pi, 2 * math.pi, ALU.is_lt, ALU.mult, R=R_, W=R_)
        tt(th, th, w1, ALU.add, R=R_, W=R_)
        act(sn, th, AF.Sin, R=R_, W=R_)
        stt(w1, th, -1.0, th, ALU.mult, ALU.max, R=R_, W=R_)
        ts(w1, w1, -1.0, math.pi / 2, ALU.mult, ALU.add, R=R_, W=R_)
        act(cs, w1, AF.Sin, R=R_, W=R_)
        tt(w1, lr, dtb, ALU.mult, R=R_, W=R_)
        act(mag, w1, AF.Exp, R=R_, W=R_)
        ar = Pr3[:, 1, :]; ai = Pi3[:, 1, :]
        memset(Pr3[:, 0, :], 1.0, W=R_); memset(Pi3[:, 0, :], 0.0, W=R_)
        tt(ar, mag, cs, ALU.mult, R=R_, W=R_)
        tt(ai, mag, sn, ALU.mult, R=R_, W=R_)
        for k in range(2, 9):
            tt(w1, Pr3[:, k - 1, :], ar, ALU.mult, R=R_, W=R_)
            tt(w2, Pi3[:, k - 1, :], ai, ALU.mult, R=R_, W=R_)
            tt(Pr3[:, k, :], w1, w2, ALU.subtract, R=R_, W=R_)
            tt(w1, Pr3[:, k - 1, :], ai, ALU.mult, R=R_, W=R_)
            tt(w2, Pi3[:, k - 1, :], ar, ALU.mult, R=R_, W=R_)
            tt(Pi3[:, k, :], w1, w2, ALU.add, R=R_, W=R_)
        tt(w1, lr, lr, ALU.mult, R=R_, W=R_)
        tt(w2, li, li, ALU.mult, R=R_, W=R_)
        tt(w1, w1, w2, ALU.add, R=R_, W=R_)
        recip(w3, w1, R=R_, W=R_)
        ts(w1, ar, -1.0, None, ALU.add, None, R=R_, W=R_)
        tt(w2, w1, lr, ALU.mult, R=R_, W=R_)
        tt(crr, ai, li, ALU.mult, R=R_, W=R_)
        tt(crr, crr, w2, ALU.add, R=R_, W=R_)
        tt(crr, crr, w3, ALU.mult, R=R_, W=R_)
        tt(w2, ai, lr, ALU.mult, R=R_, W=R_)
        tt(cii, w1, li, ALU.mult, R=R_, W=R_)
        tt(cii, w2, cii, ALU.subtract, R=R_, W=R_)
        tt(cii, cii, w3, ALU.mult, R=R_, W=R_)
        bc = lambda a: a.unsqueeze(2).broadcast_to([128, 64, 16])
        Bbr3 = Bbr.rearrange("p (g i) -> p g i", i=16); Bbi3 = Bbi.rearrange("p (g i) -> p g i", i=16)
        wk13 = wk1.rearrange("p (g i) -> p g i", i=16); wk23 = wk2.rearrange("p (g i) -> p g i", i=16)
        RB = [bpr, bBr, bBb, bwk]

        def cmul(outr, outi, pr, pi, xr, xi, Rl, Wl, negi=False, gs=slice(0, 64)):
            ng = gs.stop - gs.start
            bcg = lambda a: a[:, gs].unsqueeze(2).broadcast_to([128, ng, 16])
            k1 = wk13[:, gs, :]; k2 = wk23[:, gs, :]
            tt(k1, xr, bcg(pr), ALU.mult, R=Rl, W=[bwk])
            tt(k2, xi, bcg(pi), ALU.mult, R=Rl, W=[bwk])
            tt(outr, k1, k2, ALU.subtract, R=[bwk], W=Wl)
            tt(k1, xi, bcg(pr), ALU.mult, R=Rl, W=[bwk])
            tt(k2, xr, bcg(pi), ALU.mult, R=Rl, W=[bwk])
            if negi:
                stt(outi, k1, -1.0, k2, ALU.mult, ALU.subtract, R=[bwk], W=Wl)
            else:
                tt(outi, k1, k2, ALU.add, R=[bwk], W=Wl)

        cmul(Bbr3, Bbi3, crr, cii, Br3, Bi3, RB, [bBb])
        cp(Crb, Cr, R=[bCr], W=[bCb])
        ts(Cinb, Ci, -1.0, None, ALU.mult, None, R=[bCr], W=[bCb])
        cp(C1[:, 0, :], Pr3[:, 8, :], R=R_, W=[bS5c]); cp(C1[:, 1, :], Pr3[:, 8, :], R=R_, W=[bS5c])
        ts(C2[:, 0, :], Pi3[:, 8, :], -1.0, None, ALU.mult, None, R=R_, W=[bS5c])
        cp(C2[:, 1, :], Pi3[:, 8, :], R=R_, W=[bS5c])
        cp(CC3[:, 0:2, :], C1[:], R=[bS5c], W=[bS5c]); cp(CC3[:, 2:4, :], C2[:], R=[bS5c], W=[bS5c])
        memset(CC3[:, 4:6, :], 1.0, W=[bS5c])
        tt(w1, ar, ar, ALU.mult, R=R_, W=R_)
        tt(w2, ai, ai, ALU.mult, R=R_, W=R_)
        tt(w1, w1, w2, ALU.add, R=R_, W=R_)
        recip(w1, w1, R=R_, W=R_)
        tt(w2, ar, w1, ALU.mult, R=R_, W=R_)
        stt(w3, ai, -1.0, w1, ALU.mult, ALU.mult, R=R_, W=R_)
        for _ in range(2):
            tt(w1, w2, w2, ALU.mult, R=R_, W=R_)
            tt(th, w3, w3, ALU.mult, R=R_, W=R_)
            tt(w1, w1, th, ALU.subtract, R=R_, W=R_)
            stt(w3, w2, 2.0, w3, ALU.mult, ALU.mult, R=R_, W=R_)
            cp(w2, w1, R=R_, W=R_)
        cp(AI4[:, 0, :], w2, R=R_, W=[bS5c]); cp(AI4[:, 1, :], w3, R=R_, W=[bS5c])
        Yr3 = Yr.rearrange("p (g i) -> p g i", i=16); Yi3 = Yi.rearrange("p (g i) -> p g i", i=16)
        for k in range(8):
            cmul(Yr3, Yi3, Pr3[:, k, :], Pi3[:, k, :], Bbr3, Bbi3, [bpr, bBb], [bY])
            sBD, bsBD = stg[(2 * k) % 4]; sWI, bsWI = stg2[k % 2]
            for ft in range(16):
                h = ft // 8
                rows = slice(h * 64, (h + 1) * 64)
                cols = slice((ft % 8) * 128, (ft % 8 + 1) * 128)
                p, bp = nps()
                mm(p[:, :128], Yr[rows, cols], Crb[rows, cols], True, False, R=[bY, bCb], W=[bp])
                mm(p[:, :128], Yi[rows, cols], Cinb[rows, cols], False, True, R=[bY, bCb], W=[bp])
                tt(sBD[:, ft * 128:(ft + 1) * 128], p[:, :128], blk16[:], ALU.mult, R=[bp, bid], W=[bsBD])
                pt, bpt = npsT()
                tr(pt[:, 0:64], Yr[rows, cols], idb[rows, rows], R=[bY], W=[bpt])
                tr(pt[:, 64:128], Yi[rows, cols], idb[rows, rows], R=[bY], W=[bpt])
                cp(sWI[:, ft * 128:(ft + 1) * 128], pt[:, :128], R=[bpt], W=[bsWI], eng="act")
            s_ = 7 - k
            dma("sp", bd_d[:, :, k * 128:(k + 1) * 128].rearrange("f p c -> p f c"), sBD.rearrange("p (f c) -> p f c", f=16), R=[bsBD], W=[bscr])
            dma("sp", win_d[:, :, s_ * 128:(s_ + 1) * 128].rearrange("f p c -> p f c"), sWI.rearrange("p (f c) -> p f c", f=16), R=[bsWI], W=[bscr])
        Wh = LS[:, 0:8192]
        Wh5 = Wh.rearrange("p (g c t j) -> p g c t j", g=32, c=2, t=8)
        for gh in range(2):
            gs = slice(gh * 32, (gh + 1) * 32)
            for t_ in range(8):
                cmul(Wh5[:, :, 0, t_, :], Wh5[:, :, 1, t_, :], Pr3[:, t_ + 1, :], Pi3[:, t_ + 1, :], Cr3[:, gs, :], Ci3[:, gs, :],
                     [bpr, bCr], [bld, bldP], negi=True, gs=gs)
            dma("sp", wout2_d[:, gh * 8192:(gh + 1) * 8192], Wh, R=[bld, bldP], W=[bscr])
        memset(HH[0][:], 0.0, W=[bHH[0]])
        memset(HH[1][:], 0.0, W=[bHH[1]])

    A3 = bufA[:].rearrange("p (f n) -> p f n", f=16)
    B3 = bufB[:].rearrange("p (f n) -> p f n", f=16)
    scale_xa = 128.0 ** -0.5

    def project_fm(wsrc_cols, n_ftiles, N, sink):
        for f4 in range(0, n_ftiles, 4):
            nf = min(4, n_ftiles - f4)
            wt, bwt = wload(wsrc_cols(f4 * 128, nf * 128))
            for fo in range(nf):
                p, bp = nps()
                for k in range(8):
                    mm(p[:, :N], wt[:, k, fo * 128:(fo + 1) * 128], hT[:, k, :N], k == 0, k == 7, R=[bwt, bhT], W=[bp])
                sink(f4 + fo, p, bp)

    def xattn_prompt(l, N):
        ex = exP
        for h in range(4):
            e, be = ex[h % 2]
            e3 = e.rearrange("p (t n) -> p t n", t=2)
            for mt in range(2):
                p, bp = nps()
                mm(p[:, :N], kT_all[:, l, h, mt * 128:(mt + 1) * 128], qxa[:, h, :N], True, True, R=[bkT, bq], W=[bp])
                act(e3[:, mt, :N], p[:, :N], AF.Exp, R=[bp], W=[be], scale=scale_xa)
            pd, bpd = nps()
            for mt in range(2):
                mm(pd[:, :N], onesb[:], e3[:, mt, :N], mt == 0, mt == 1, R=[be, bid], W=[bpd])
            po, bpo = nps()
            for mt in range(2):
                mm(po[:, :N], v_all[:, l, mt, h * 128:(h + 1) * 128], e3[:, mt, :N], mt == 0, mt == 1, R=[be, bv], W=[bpo])
            rot()
            act(f1[:, :N], pd[:, :N], AF.Ln, R=[bpd], W=[bf1])
            act(f1[:, :N], f1[:, :N], AF.Exp, R=[bf1], W=[bf1], scale=-1.0)
            tt(qxa[:, h, :N], po[:, :N], f1[:, :N], ALU.mult, R=[bpo, bf1, bq], W=[bq])

    def xattn_sample(l):
        cvx = Carve()
        Kc = [cvx(f"Kc{i}", 1024, BF16) for i in range(4)]
        Vc = [cvx(f"Vc{i}", 1024, BF16) for i in range(4)]
        kTb = [cvx(f"kTb{i}", 1024, BF16) for i in range(2)]
        exs, bexs = cvx("exs", 512, BF16)
        ps_s, bps_s = nps()
        ps_o, bps_o = nps()
        vts = []
        for b in range(NSB):
            K_, bK = Kc[b % 4]; V_, bV = Vc[b % 4]; kt_, bkt = kTb[b % 2]
            K3 = K_.rearrange("p (t f) -> p t f", t=2); V3 = V_.rearrange("p (t f) -> p t f", t=2)
            dma("pool", K3, ck[l, b].rearrange("(t p) f -> p t f", p=128), W=[bK])
            dma("pool", V3, cv[l, b].rearrange("(t p) f -> p t f", p=128), W=[bV])
            pt, bpt = npsT()
            for h in range(4):
                for mt in range(2):
                    tr(pt[:, h * 256 + mt * 128: h * 256 + (mt + 1) * 128], K3[:, mt, h * 128:(h + 1) * 128], idb[:], R=[bK], W=[bpt])
            cp(kt_, pt[:], R=[bpt], W=[bkt], eng="act")
            for h in range(4):
                for mt in range(2):
                    c0 = mt * 256 + b * 16 + h * 4
                    mm(ps_s[:, c0:c0 + 4], kt_[:, h * 256 + mt * 128: h * 256 + (mt + 1) * 128], qxa[:, h, b * 4:(b + 1) * 4],
                       True, True, R=[bkt, bq], W=[bps_s])
            vts.append((V3, bV, b))
            for mt in range(2):
                c0 = mt * 256 + b * 16
                act(exs[:, c0:c0 + 16], ps_s[:, c0:c0 + 16], AF.Exp, R=[bps_s], W=[bexs], scale=scale_xa)
            for h in range(4):
                for mt in range(2):
                    c0 = mt * 256 + b * 16 + h * 4
                    mm(ps_o[:, b * 16 + h * 4: b * 16 + h * 4 + 4], V3[:, mt, h * 128:(h + 1) * 128], exs[:, c0:c0 + 4],
                       mt == 0, mt == 1, R=[bV, bexs], W=[bps_o])
        pd, bpd = nps()
        for mt in range(2):
            mm(pd[:, :256], onesb[:], exs[:, mt * 256:(mt + 1) * 256], mt == 0, mt == 1, R=[bexs, bid], W=[bpd])
        recip(f1[:, :256], pd[:, :256], R=[bpd], W=[bf1])
        tt(qxa[:, :, :64].rearrange("p h (b t) -> p b h t", b=NSB),
           ps_o[:, :256].rearrange("p (b h t) -> p b h t", b=NSB, h=4),
           f1[:, :256].rearrange("p (b h t) -> p b h t", b=NSB, h=4), ALU.mult, R=[bps_o, bf1, bq], W=[bq])

    def out_proj(l, N, rows_list, ycat):
        for dh in range(2):
            pl = [nps() for _ in rows_list]
            for u in range(3):
                kt = 8 if u < 2 else 4
                wt, bwt = wload(w_out[l, u * 1024:u * 1024 + kt * 128, dh * 512:(dh + 1) * 512])
                for ti, (tix, rows, c0) in enumerate(rows_list):
                    p, bp = pl[ti]
                    for k in range(kt):
                        ft = u * 8 + k
                        ya, by = ycat(ft)
                        mm(p[:rows, :], ya[:, c0:c0 + rows], wt[:, k, :], ft == 0, ft == 19, R=[bwt, by], W=[bp])
            for ti, (tix, rows, c0) in enumerate(rows_list):
                p, bp = pl[ti]
                xa_ = x_sb[:rows, tix, dh * 512:(dh + 1) * 512]
                tt(xa_, xa_, p[:rows, :], ALU.add, R=[bp, bx], W=[bx])

    def layer_sgu(l, j, N, tiles, sample):
        cvl = Carve()
        WtT, bWt = cvl("WtT", 1024, F32)
        biasB, bbias = cvl("biasB", 1024, F32)
        gsg, bgsg = cvl("gsg", E, F32)
        Wr = [cvl(f"Wr{i}", 1024, BF16) for i in range(2)]
        ssv, bssv = cvl("ssv", 32, F32)
        vraw, bvraw = cvl("vraw", E, F32)
        wq = w_in_a[j]
        dma("sp", gsg, sgu_norm_g[j:j + 1, :].partition_broadcast(128), W=[bgsg])
        WtT3 = WtT.rearrange("p (g t) -> p g t", g=8)
        bias3 = biasB.rearrange("p (g t) -> p g t", g=8)
        if not sample:
            CH = 128
            dma("sp", WtT3, sgu_wT[j].rearrange("g s t -> s g t"), W=[bWt])
            tri, btri = cvl("tri", 128, F32)
            dma("sp", tri, c_triu, W=[btri])
            tt(WtT3, WtT3, tri.unsqueeze(1).broadcast_to([128, 8, 128]), ALU.mult, R=[bWt, btri], W=[bWt])
            dma("sp", biasB, sgu_b[j:j + 1, :].partition_broadcast(128), W=[bbias])
        else:
            CH = 64
            w4, bw4 = cvl("w4", 32, F32)
            sel, bsel = cvl("sel", 64, F32)
            m64, bm64 = cvl("m64", 64, F32)
            dma("sp", sel[:4, :], c_sel, W=[bsel])
            dma("sp", m64[:64, :], c_m64, W=[bm64])
            dma("sp", w4[:4, :].rearrange("s (g t) -> s g t", g=8), sgu_wT[j, :, 0:4, 0:4].rearrange("g s t -> s g t"), W=[bw4])
            p, bp = nps()
            mm(p[:64, :32], sel[:4, :], w4[:4, :], True, True, R=[bsel, bw4], W=[bp])
            wrep, bwrep = cvl("wrep", 32, F32)
            cp(wrep[:64, :], p[:64, :32], R=[bp], W=[bwrep])
            Wb4 = WtT[:64, :512].rearrange("p (g b t) -> p g b t", g=8, b=16)
            tt(Wb4, wrep[:64, :].rearrange("p (g t) -> p g t", g=8).unsqueeze(2).broadcast_to([64, 8, 16, 4]),
               m64[:64, :].rearrange("p (b t) -> p b t", b=16).unsqueeze(1).broadcast_to([64, 8, 16, 4]), ALU.mult,
               R=[bwrep, bm64], W=[bWt])
            b4, bb4 = cvl("b4", 32, F32)
            dma("sp", b4.rearrange("p (g t) -> p g t", g=8),
                sgu_b[j].rearrange("(g t) -> g t", g=8)[:, 0:4].partition_broadcast(128), W=[bb4])
            tt(biasB[:, :512].rearrange("p (g b t) -> p g b t", g=8, b=16),
               b4.rearrange("p (g t) -> p g t", g=8).unsqueeze(2).broadcast_to([128, 8, 16, 4]),
               b4.rearrange("p (g t) -> p g t", g=8).unsqueeze(2).broadcast_to([128, 8, 16, 4]), ALU.max, R=[bb4], W=[bbias])
            WtT3 = WtT[:, :512].rearrange("p (g t) -> p g t", g=8)
            bias3 = biasB[:, :512].rearrange("p (g t) -> p g t", g=8)
        vtok = bufA[:, :4 * E].rearrange("p (t f) -> p t f", t=4)
        for cb in range(4):
            wt, bwt = wload(wq[:, E + cb * 512:E + (cb + 1) * 512])
            for ti, (tix, rows, c0) in enumerate(tiles):
                p, bp = nps()
                for k in range(8):
                    mm(p[:rows, :], hT[:, k, c0:c0 + rows], wt[:, k, :], k == 0, k == 7, R=[bwt, bhT], W=[bp])
                act(junk[:rows, :512], p[:rows, :], AF.Square, R=[bp], W=[bjunk, bssv], accum_out=ssv[:rows, ti * 4 + cb:ti * 4 + cb + 1])
                tt(vtok[:rows, ti, cb * 512:(cb + 1) * 512], p[:rows, :], gsg[:rows, cb * 512:(cb + 1) * 512], ALU.mult, R=[bp, bgsg], W=[bA])
                if sample:
                    tt(vraw[:rows, cb * 512:(cb + 1) * 512], p[:rows, :], gsg[:rows, cb * 512:(cb + 1) * 512], ALU.mult, R=[bp, bgsg], W=[bvraw])
        nt = len(tiles)
        S.op("dve", lambda e: e.tensor_reduce(out=ssv[:, 16:16 + nt], in_=ssv[:, :4 * nt].rearrange("p (t c) -> p t c", c=4),
                                              axis=mybir.AxisListType.X, op=ALU.add), R=[bssv], W=[bssv])
        rstd(ssv[:, 24:24 + nt], ssv[:, 16:16 + nt], E, R=[bssv], W=[bssv])
        if sample:
            ts(vraw[:64, :], vraw[:64, :], ssv[:64, 24:25], None, ALU.mult, None, R=[bvraw, bssv], W=[bvraw])
            final_ops.append(dma("sp", cv_s[j], vraw[:64, :], R=[bvraw]))
        wrs = []
        for ti, (tix, rows, c0) in enumerate(tiles):
            w_, bw_ = Wr[ti % 2] if nt > 2 else Wr[ti % 2]
            wrs.append((w_, bw_))
        if not sample:
            extra = [vraw[:, 0:512].bitcast(BF16), vraw[:, 512:1024].bitcast(BF16)]
            wrs = [Wr[0], Wr[1], (extra[0], bvraw), (extra[1], bvraw)]
        for ti, (tix, rows, c0) in enumerate(tiles):
            w_, bw_ = wrs[ti]
            ts(w_[:rows, :8 * CH], WtT[:rows, :8 * CH], ssv[:rows, 24 + ti:25 + ti], None, ALU.mult, None, R=[bWt, bssv], W=[bw_])
        for f4 in range(4):
            wu, bwu = wload(wq[:, f4 * 512:(f4 + 1) * 512])
            wz, bwz = wload(wq[:, 2 * E + f4 * 512:2 * E + (f4 + 1) * 512])
            for fo in range(4):
                f = f4 * 4 + fo
                g = f // 2
                pu, bpu = nps()
                for k in range(8):
                    mm(pu[:, :N], wu[:, k, fo * 128:(fo + 1) * 128], hT[:, k, :N], k == 0, k == 7, R=[bwu, bhT], W=[bpu])
                pz, bpz = nps()
                for k in range(8):
                    mm(pz[:, :N], wz[:, k, fo * 128:(fo + 1) * 128], hT[:, k, :N], k == 0, k == 7, R=[bwz, bhT], W=[bpz])
                pm, bpm = nps()
                for ti, (tix, rows, c0) in enumerate(tiles):
                    w_, bw_ = wrs[ti]
                    mm(pm[:, c0:c0 + rows], vtok[:rows, ti, f * 128:(f + 1) * 128], w_[:rows, g * CH:(g + 1) * CH], True, True, R=[bA, bw_], W=[bpm])
                rot()
                act(t1[:, :N], pz[:, :N], AF.Silu, R=[bpz], W=[bt1])
                tt(t2[:, :N], pu[:, :N], t1[:, :N], ALU.mult, R=[bpu, bt1], W=[bt2])
                tt(f1[:, :N].rearrange("p (c t) -> p c t", t=CH), pm[:, :N].rearrange("p (c t) -> p c t", t=CH),
                   bias3[:, g, :].unsqueeze(1).broadcast_to([128, N // CH, CH]), ALU.add, R=[bpm, bbias], W=[bf1])
                tt(B3[:, f, :N], t2[:, :N], f1[:, :N], ALU.mult, R=[bt2, bf1], W=[bB])
        project_fm(lambda c0, n: wq[:, 3 * E + c0:3 * E + c0 + n], 4, N,
                   lambda f, p, bp: cp(qxa[:, f, :N], p[:, :N], R=[bp], W=[bq], eng="act"))

    def layer_conv(l, N, tiles, sample, last_prompt):
        cvl = Carve()
        c32, bc32 = cvl("c32", 16 * 64, F32)
        c32_3 = c32.rearrange("p (f n) -> p f n", f=16)
        W_ = 6 if sample else 1
        if sample:
            cs3 = A3[:, :, :96].rearrange("p f (b w) -> p f b w", w=6)
            cdst = lambda f: cs3[:, f, :, 2:6]
            sct, bsct = cvl("sct", E, F32, parts=32)
            dma("sp", sct[:32, :], sconv, W=[bsct])
            for f4 in range(4):
                p, bp = nps()
                for fo in range(4):
                    f = f4 * 4 + fo
                    tr(p[:, fo * 32:(fo + 1) * 32], sct[:32, f * 128:(f + 1) * 128], idf[:32, :32], R=[bsct], W=[bp])
                cp(cs3[:, f4 * 4:(f4 + 1) * 4, :, 0:2], p[:, :128].rearrange("p (f b w) -> p f b w", f=4, w=2), R=[bp], W=[bA])
            view = lambda ap: ap.rearrange("p (b t) -> p b t", t=4)
            tap = lambda f, k: cs3[:, f, :, k:k + 4]
        else:
            cp(A3[:, :, 0:2], halo[:], R=[bhalo], W=[bA])
            cdst = lambda f: A3[:, f, 2:2 + N]
            view = lambda ap: ap
            tap = lambda f, k: A3[:, f, k:k + N]
        project_fm(lambda c0, n: w_in_b[:, c0:c0 + n], 16, N,
                   lambda f, p, bp: cp(B3[:, f, :N], p[:, :N], R=[bp], W=[bB], eng="act"))
        project_fm(lambda c0, n: w_in_b[:, E + c0:E + c0 + n], 16, N,
                   lambda f, p, bp: cp(cdst(f), view(p[:, :N]), R=[bp], W=[bA], eng="act"))

        def hv_sink(f, p, bp):
            if sample:
                tt(c32_3[:, f, :64].rearrange("p (b t) -> p b t", t=4), cdst(f), view(p[:, :N]), ALU.mult, R=[bp, bA], W=[bc32])
            elif last_prompt:
                tt(c32_3[:, f, 0:2], A3[:, f, N:N + 2], p[:, N - 2:N], ALU.mult, R=[bp, bA], W=[bc32])
            tt(cdst(f), cdst(f), view(p[:, :N]), ALU.mult, R=[bp, bA], W=[bA])
        project_fm(lambda c0, n: w_in_b[:, 2 * E + c0:2 * E + c0 + n], 16, N, hv_sink)
        if not sample:
            cp(halo[:], A3[:, :, N:N + 2], R=[bA], W=[bhalo])

        def z_sink(f, p, bp):
            rot()
            act(t1[:, :N], p[:, :N], AF.Silu, R=[bp], W=[bt1])
            act(view(f1[:, :N]), tap(f, 0), AF.Copy, R=[bA], W=[bf1], scale=wcv[:, f:f + 1])
            stt(view(f1[:, :N]), tap(f, 1), wcv[:, 16 + f:17 + f], view(f1[:, :N]), ALU.mult, ALU.add, R=[bA, bf1], W=[bf1])
            stt(view(f1[:, :N]), tap(f, 2), wcv[:, 32 + f:33 + f], view(f1[:, :N]), ALU.mult, ALU.add, R=[bA, bf1], W=[bf1])
            tt(t2[:, :N], B3[:, f, :N], t1[:, :N], ALU.mult, R=[bB, bt1], W=[bt2], eng="pool")
            tt(B3[:, f, :N], t2[:, :N], f1[:, :N], ALU.mult, R=[bt2, bf1, bB], W=[bB])
        project_fm(lambda c0, n: w_in_b[:, 3 * E + c0:3 * E + c0 + n], 16, N, z_sink)
        project_fm(lambda c0, n: w_in_b[:, 4 * E + c0:4 * E + c0 + n], 4, N,
                   lambda f, p, bp: cp(qxa[:, f, :N], p[:, :N], R=[bp], W=[bq], eng="act"))
        if sample or last_prompt:
            nco = 64 if sample else 2
            ost, bost = cvl("ost", E, F32, parts=64)
            for f4 in range(4):
                p, bp = nps()
                for fo in range(4):
                    f = f4 * 4 + fo
                    tr(p[:nco, fo * 128:(fo + 1) * 128], c32_3[:, f, :nco], idf[:], R=[bc32], W=[bp])
                cp(ost[:nco, f4 * 512:(f4 + 1) * 512], p[:nco, :], R=[bp], W=[bost])
            if sample:
                for b in range(NSB):
                    final_ops.append(dma("sp", conv_s[b], ost[b * 4 + 2:b * 4 + 4, :], R=[bost]))
            else:
                final_ops.append(dma("sp", conv_p, ost[:2, :], R=[bost]))

    def layer_s5(l, N, tiles, sample, last_prompt):
        cvl = Carve()
        NC_ = 16 if sample else N // 8
        NN = NC_ * 8
        ums = [cvl(f"um{i}", 8 * NN, BF16) for i in range(2)]
        umb = [[S.buf(f"um{i}_{g}", ls=True) for g in range(8)] for i in range(2)]
        SH, bSH = cvl("SH", 2 * 64 * NC_, BF16)
        NB5 = 3 if sample else 2
        BDb = [cvl(f"BDb{i}", 1024, BF16) for i in range(NB5)]
        Wib = [cvl(f"Wib{i}", 1024, BF16) for i in range(NB5)]
        NWO = 2 if sample else 1
        Wo2s = [cvl(f"Wo2{i}", 8 * 256, BF16) for i in range(NWO)]
        Yts = [cvl(f"Yt{i}", 1024, BF16, parts=64) for i in range(2)]
        xw1, bxw = cvl("xw1", 1024 if sample else 256, F32)
        xw2 = cvl("xw2", 1024, F32)[0] if sample else None
        SH4 = SH.rearrange("p (c g n) -> p c g n", c=2, g=64)
        if sample:
            HI4, bHI = SH4, bSH
        else:
            HI, bHI = cvl("Hhist", 2 * 64 * NC_, BF16)
            HI4 = HI.rearrange("p (c g n) -> p c g n", c=2, g=64)
        bAf = [S.buf(f"A_f{i}", ls=True) for i in range(16)]

        def zq_phase(part=None):
            for pt_ in (range(4) if part is None else ([part] if part < 4 else [])):
                project_fm(lambda c0, n, pt_=pt_: w_in_c[:, E + pt_ * 512 + c0:E + pt_ * 512 + c0 + n], 4, N,
                           lambda f, p, bp, pt_=pt_: act(B3[:, pt_ * 4 + f, :N], p[:, :N], AF.Silu, R=[bp], W=[bB]))
            if part is None or part == 4:
                project_fm(lambda c0, n: w_in_c[:, 2 * E + c0:2 * E + c0 + n], 4, N,
                           lambda f, p, bp: cp(qxa[:, f, :N], p[:, :N], R=[bp], W=[bq], eng="act"))
                if not sample:
                    xattn_prompt(l, N)

        if sample:
            memset(A3[:, :, :128], 0.0, W=bAf + [bA])
            usink = lambda f, p, bp: cp(A3[:, f, :128].rearrange("p (s b) -> p s b", s=8)[:, 4:8, :],
                                        p[:, :64].rearrange("p (b t) -> p t b", t=4), R=[bp], W=[bAf[f]], eng="act")
        else:
            usink = lambda f, p, bp: cp(A3[:, f, :N].rearrange("p (s n) -> p s n", s=8),
                                        p[:, :N].rearrange("p (n s) -> p s n", s=8), R=[bp], W=[bAf[f]], eng="act")
        project_fm(lambda c0, n: w_in_c[:, c0:c0 + n], 16, N, usink)
        mark(f's5 loop1')
        for ft in range(16):
            h = ft // 8
            wi, bwi = Wib[ft % NB5]
            dma("sp", wi, win_d[ft], R=[bscr], W=[bwi])
            wi4 = wi.rearrange("p (s c q) -> p s c q", s=8, c=2)
            um, _ = ums[ft % 2]
            bum8 = umb[ft % 2]
            um3 = um.rearrange("p (g n) -> p g n", g=8)
            for g8 in range(8):
                ts(um3[:, g8, :NN], A3[:, ft, :NN], mask8[:, g8:g8 + 1], None, ALU.mult, None, R=[bAf[ft]], W=[bum8[g8]])
            pS = [nps() for _ in range(2)]
            for c in range(2):
                p, bp = pS[c]
                for s in range(8):
                    mm(p[h * 64:(h + 1) * 64, :8 * NC_].rearrange("p (g n) -> p g n", g=8), wi4[:, s, c, :], um3[:, :, s * NC_:(s + 1) * NC_],
                       s == 0, s == 7, R=[bwi] + bum8, W=[bp], tile_position=(0, h * 64))
            gl = (ft % 8) * 8
            for c in range(2):
                p, bp = pS[c]
                cp(SH4[h * 64:(h + 1) * 64, c, gl:gl + 8, :NC_], p[h * 64:(h + 1) * 64, :8 * NC_].rearrange("p (g n) -> p g n", g=8),
                   R=[bp], W=[bSH], eng="act")
        mark(f's5 recurrence')
        if not sample:
            a0 = s5state["cur"]
            cp(HH[a0][:, 4:6, :], SH4[:, :, :, 0], R=[bSH], W=[bHS[a0]], eng="act")
            for c in range(NC_):
                a = s5state["cur"]; b_ = 1 - a
                Ha, Hb = HH[a], HH[b_]
                cp(HI4[:, :, :, c], Ha[:, 0:2, :], R=[bHH[a]], W=[bHI], eng="act")
                if c + 1 < NC_:
                    cp(Hb[:, 4:6, :], SH4[:, :, :, c + 1], R=[bSH], W=[bHS[b_]], eng="act")
                tt(Pk[:], Ha[:], CC3[:], ALU.mult, R=[bHH[a], bHS[a], bS5c], W=[bPk])
                S.op("dve", lambda e, Hb=Hb: e.tensor_reduce(out=Hb[:, 0:2, :].rearrange("p c g -> p (c g)"),
                                                             in_=Pk[:].rearrange("p (k c) g -> p (c g) k", k=3),
                                                             axis=mybir.AxisListType.X, op=ALU.add), R=[bPk], W=[bHH[b_]])
                h2 = Hb[:, 0:2, :]
                rev = type(h2)(h2.tensor, h2.offset + 64, [[h2.ap[0][0], 128], [-64, 2], [1, 64]])
                cp(Hb[:, 2:4, :], rev, R=[bHH[b_]], W=[bHH[b_]])
                s5state["cur"] = b_
                if c >= 4 and (c - 4) % 8 == 0 and (c - 4) // 8 < 5:
                    zq_phase((c - 4) // 8)
            if last_prompt:
                Hf = HH[s5state["cur"]]; bHf = bHH[s5state["cur"]]
                for c, dst in enumerate((s5r_p, s5i_p)):
                    p, bp = nps()
                    tr(p[:64, :128], Hf[:, c, :], idf[:], R=[bHf], W=[bp])
                    cp(xw1[:64, c * 128:(c + 1) * 128], p[:64, :128], R=[bp], W=[bxw])
                    final_ops.append(dma("sp", dst.rearrange("(h g) q -> g h q", h=2),
                                         xw1[:64, c * 128:(c + 1) * 128].rearrange("g (h q) -> g h q", h=2), R=[bxw]))
        else:
            zq_phase()
            h0, bh0 = cvl("h0", 2 * 64 * 16, F32)
            h04 = h0.rearrange("p (c g b) -> p c g b", c=2, g=64)
            ldS, bldS = cvl("ldS", 16 * 128, F32, parts=64)
            for c, src in enumerate((s5r0, s5i0)):
                dma("sp", ldS.rearrange("g (b h q) -> g b h q", b=NSB, h=2), src.rearrange("b (h g) q -> g b h q", h=2), W=[bldS])
                for b4 in range(4):
                    p, bp = nps()
                    for bb in range(4):
                        b = b4 * 4 + bb
                        tr(p[:, bb * 64:(bb + 1) * 64], ldS[:64, b * 128:(b + 1) * 128], idf[:64, :64], R=[bldS], W=[bp])
                    cp(h04[:, c, :, b4 * 4:(b4 + 1) * 4], p[:, :256].rearrange("p (b g) -> p g b", b=4), R=[bp], W=[bh0])
            X4 = xw1.rearrange("p (c g b) -> p c g b", c=1, g=64); Y4 = xw2.rearrange("p (c g b) -> p c g b", c=1, g=64)
            bcb = lambda a: a.unsqueeze(3).broadcast_to([128, 2, 64, 16])
            bcb1 = lambda a: a.unsqueeze(2).broadcast_to([128, 64, 16])
            Hp, bHp = cvl("Hp", 2 * 64 * 16, F32)
            Hp4 = Hp.rearrange("p (c g b) -> p c g b", c=2, g=64)
            ar4 = AI4[:, 0, :]; ai4 = AI4[:, 1, :]
            tt(X4[:, 0], h04[:, 0], bcb1(ar4), ALU.mult, R=[bh0, bS5c], W=[bxw])
            tt(Y4[:, 0], h04[:, 1], bcb1(ai4), ALU.mult, R=[bh0, bS5c], W=[bxw])
            tt(Hp4[:, 0], X4[:, 0], Y4[:, 0], ALU.subtract, R=[bxw], W=[bHp])
            tt(X4[:, 0], h04[:, 1], bcb1(ar4), ALU.mult, R=[bh0, bS5c], W=[bxw])
            tt(Y4[:, 0], h04[:, 0], bcb1(ai4), ALU.mult, R=[bh0, bS5c], W=[bxw])
            tt(Hp4[:, 1], X4[:, 0], Y4[:, 0], ALU.add, R=[bxw], W=[bHp])
            a8r = C1[:, 0, :]; a8i = C2[:, 1, :]
            Hf, bHf_ = h0, bh0
            Hf4 = Hf.rearrange("p (c g b) -> p c g b", c=2, g=64)
            tt(X4[:, 0], Hp4[:, 0], bcb1(a8r), ALU.mult, R=[bHp, bS5c], W=[bxw])
            tt(Y4[:, 0], Hp4[:, 1], bcb1(a8i), ALU.mult, R=[bHp, bS5c], W=[bxw])
            tt(X4[:, 0], X4[:, 0], Y4[:, 0], ALU.subtract, R=[bxw], W=[bxw])
            tt(Hf4[:, 0], X4[:, 0], SH4[:, 0, :, :16], ALU.add, R=[bxw, bSH], W=[bHf_])
            tt(X4[:, 0], Hp4[:, 1], bcb1(a8r), ALU.mult, R=[bHp, bS5c], W=[bxw])
            tt(Y4[:, 0], Hp4[:, 0], bcb1(a8i), ALU.mult, R=[bHp, bS5c], W=[bxw])
            tt(X4[:, 0], X4[:, 0], Y4[:, 0], ALU.add, R=[bxw], W=[bxw])
            tt(Hf4[:, 1], X4[:, 0], SH4[:, 1, :, :16], ALU.add, R=[bxw, bSH], W=[bHf_])
            cp(SH4[:, :, :, :16], Hp4, R=[bHp, bSH], W=[bSH])
            for c, dst in enumerate((s5r_s, s5i_s)):
                for b4 in range(4):
                    p, bp = nps()
                    for bb in range(4):
                        b = b4 * 4 + bb
                        tr(p[:64, bb * 128:(bb + 1) * 128], Hf4[:, c, :, b], idf[:], R=[bHf_], W=[bp])
                    cp(ldS[:64, b4 * 512:(b4 + 1) * 512], p[:64, :], R=[bp], W=[bldS])
                final_ops.append(dma("sp", dst.rearrange("b (h g) q -> g b h q", h=2),
                                     ldS.rearrange("g (b h q) -> g b h q", b=NSB, h=2), R=[bldS]))
        mark(f's5 loop2')
        for ft in range(16):
            h = ft // 8
            gl = (ft % 8) * 8
            bd, bbd = BDb[ft % NB5]
            Wo2, bWo2 = Wo2s[ft % NWO]
            hr_ = slice(h * 64, (h + 1) * 64)
            dma("sp", bd, bd_d[ft], R=[bscr], W=[bbd])
            dma("sp", Wo2[hr_, :], wout2_d[hr_, gl * 256:(gl + 8) * 256], R=[bscr], W=[bWo2])
            bd3 = bd.rearrange("p (t q) -> p t q", t=8)
            wo4 = Wo2.rearrange("p (g c f) -> p g c f", g=8, c=2)
            pst = [nps() for _ in range(2)]
            Yt, bYt = Yts[ft % 2]
            for g8 in range(8):
                for th in range(2):
                    p_, bp_ = pst[th]
                    for c in range(2):
                        mm(p_[:NC_, :].rearrange("n (t g j) -> n t g j", t=4, g=8)[:, :, g8, :], HI4[hr_, c, gl + g8, :NC_],
                           wo4[hr_, g8, c, th * 64:(th + 1) * 64].rearrange("p (t j) -> p t j", t=4), c == 0, c == 1,
                           R=[bWo2, bHI], W=[bp_], tile_position=(h * 64, 0))
            for th in range(2):
                p_, bp_ = pst[th]
                cp(Yt[:NC_, th * 512:(th + 1) * 512], p_[:NC_, :], R=[bp_], W=[bYt], eng="act" if th == 0 else "dve")
            py, bpy = nps()
            py_sm = py[:, :NN].rearrange("p (s n) -> p s n", s=8)
            u_sm = A3[:, ft, :NN].rearrange("p (s n) -> p s n", s=8)
            for tau in range(8):
                mm(py[:, tau * NC_:NN], bd3[:, tau, :], A3[:, ft, 0:NN - tau * NC_], tau == 0, False, R=[bbd, bAf[ft]], W=[bpy])
            for t_ in range(8):
                mm(py_sm[:, t_, :], Yt[:NC_, t_ * 128:(t_ + 1) * 128], idb[:NC_, :NC_], False, t_ == 7, R=[bYt, bid], W=[bpy],
                   skip_group_check=True)
            rot()
            if sample:
                src_y = py_sm[:, 4:8, :]
                src_u = u_sm[:, 4:8, :]
                o1 = f1[:, :64].rearrange("p (b t) -> p t b", t=4)
            else:
                src_y = py_sm; src_u = u_sm
                o1 = f1[:, :N].rearrange("p (n s) -> p s n", s=8)
            stt(o1, src_u, dcol[:, ft:ft + 1], src_y, ALU.mult, ALU.add, R=[bAf[ft], bpy], W=[bf1])
            act(A3[:, ft, :N], f1[:, :N], AF.Gelu_apprx_tanh, R=[bf1], W=[bAf[ft]])
        mark(f's5 glu')
        for f4 in range(4):
            wa, bwa = wload(w_glu[0:1024, f4 * 512:(f4 + 1) * 512])
            wb, bwb = wload(w_glu[1024:2048, f4 * 512:(f4 + 1) * 512])
            for fo in range(4):
                f = f4 * 4 + fo
                p, bp = nps()
                for k in range(16):
                    wt_, bwt_ = (wa, bwa) if k < 8 else (wb, bwb)
                    mm(p[:, :N], wt_[:, k % 8, fo * 128:(fo + 1) * 128], A3[:, k, :N], k == 0, k == 15, R=[bwt_, bAf[k]], W=[bp])
                rot()
                act(t1[:, :N], p[:, :N], AF.Sigmoid, R=[bp], W=[bt1], bias=bgl[:, f:f + 1])
                tt(t2[:, :N], A3[:, f, :N], t1[:, :N], ALU.mult, R=[bAf[f], bt1], W=[bt2])
                tt(B3[:, f, :N], t2[:, :N], B3[:, f, :N], ALU.mult, R=[bt2, bB], W=[bB])

    if blocks is None:
        blocks = [("p", i) for i in range(4)] + [("s", 0)]
    for kind, bi in blocks:
        sample = kind == "s"
        if sample:
            N = NS
            tiles = [(0, 64, 0)]
            dma("sp", x_sb[:64, 0, :], xs, W=[bx])
        else:
            N = 512
            tiles = [(t, 128, t * 128) for t in range(4)]
            dma("sp", x_sb[:], xp[bi * 512:(bi + 1) * 512, :].rearrange("(t p) d -> p t d", p=128), W=[bx])
        last_prompt = (not sample) and bi == 3
        wstate["id"] = 0
        wstate["first"] = (kind, bi) == blocks[0]
        for l in range(n_layers):
            mark(f"{kind}{bi} L{l} start")
            dma("sp", gB[:], norm_g[l:l + 1, :].partition_broadcast(128), W=[bgB])
            barrier()
            norm_transpose_multi([(x_sb[:rows, tix, :], rows, hT[:, :, c0:c0 + rows]) for (tix, rows, c0) in tiles], bx, gB, bgB, bhT)
            kd = KINDS[l]
            if kd == 0:
                layer_sgu(l, KIDX[l], N, tiles, sample)
                ycat = lambda ft: ((B3[:, ft, :], bB) if ft < 16 else (qxa[:, ft - 16, :], bq))
            elif kd == 1:
                layer_conv(l, N, tiles, sample, last_prompt)
                ycat = lambda ft: ((B3[:, ft, :], bB) if ft < 16 else (qxa[:, ft - 16, :], bq))
            else:
                if do_s5:
                    layer_s5(l, N, tiles, sample, last_prompt)
                    ycat = lambda ft: ((B3[:, ft, :], bB) if ft < 16 else (qxa[:, ft - 16, :], bq))
                else:
                    continue
            mark(f"{kind}{bi} L{l} xattn")
            if sample:
                barrier()
                xattn_sample(l)
            elif kd != 2:
                xattn_prompt(l, N)
            mark(f"{kind}{bi} L{l} outproj")
            out_proj(l, N, tiles, ycat)
        dma("sp", gB[:], final_g.partition_broadcast(128), W=[bgB])
        for i, (tix, rows, c0) in enumerate(tiles):
            act(junk[:rows], x_sb[:rows, tix, :], AF.Square, R=[bx], W=[bjunk, bsm], accum_out=sm[:rows, i:i + 1])
        rstd(sm[:tiles[0][1], 8:8 + len(tiles)], sm[:tiles[0][1], 0:len(tiles)], D, R=[bsm], W=[bsm])
        ystg = bufB[:, :4 * 2 * D].bitcast(F32).rearrange("p (t d) -> p t d", t=4)
        for i, (tix, rows, c0) in enumerate(tiles):
            stt(ystg[:rows, tix, :], x_sb[:rows, tix, :], sm[:rows, 8 + i:9 + i], gB[:rows, :], ALU.mult, ALU.mult, R=[bx, bsm, bgB], W=[bB])
        if sample:
            final_ops.append(dma("sp", y_s, ystg[:64, 0, :], R=[bB]))
        else:
            final_ops.append(dma("sp", y_p[bi * 512:(bi + 1) * 512, :].rearrange("(t p) d -> p t d", p=128), ystg, R=[bB]))

    mark('end')
    S.emit(nc, es, final_ops)
    es.close()
    return nc


_CACHE = {}


def _consts():
    ident = np.eye(128, dtype=np.float32)
    triu = np.triu(np.ones((128, 128), dtype=np.float32))
    blk16 = np.kron(np.eye(8, dtype=np.float32), np.ones((16, 16), dtype=np.float32))
    m64 = np.kron(np.eye(16, dtype=np.float32), np.triu(np.ones((4, 4), dtype=np.float32)))
    sel = np.tile(np.eye(4, dtype=np.float32), (1, 16))
    mask8 = np.kron(np.eye(8, dtype=np.float32), np.ones((16, 1), dtype=np.float32))
    return dict(c_ident=ident, c_triu=triu, c_blk16=blk16, c_m64=m64, c_sel=sel, c_mask8=mask8)


def kernel(x_prompt, x_sample, mem_prompt, cache_mem_k, cache_mem_v, state_conv, state_s5_re, state_s5_im,
           norm_g, final_g, mem_norm_g, w_kv, w_out, w_in_a, sgu_norm_g, sgu_w, sgu_b, w_in_b, conv_w, w_in_c,
           s5_lam_re, s5_lam_im, s5_log_dt, s5_b_re, s5_b_im, s5_c_re, s5_c_im, s5_d, w_glu, b_glu):
    f = lambda a: np.ascontiguousarray(np.asarray(a, dtype=np.float32))
    if "nc" not in _CACHE:
        _CACHE["nc"] = build_program()
    nc = _CACHE["nc"]
    shared = dict(
        norm_g=f(norm_g), final_g=f(final_g).reshape(1, D), mem_norm_g=f(mem_norm_g), w_kv=f(w_kv), w_out=f(w_out),
        w_in_a=f(w_in_a), sgu_norm_g=f(sgu_norm_g), sgu_wT=f(np.transpose(np.asarray(sgu_w), (0, 1, 3, 2))),
        sgu_b=f(sgu_b).reshape(2, 1024), w_in_b=f(w_in_b)[0], conv_w=f(np.asarray(conv_w)[0].reshape(3, 16, 128).transpose(2, 0, 1).reshape(128, 48)),
        w_in_c=f(w_in_c)[0], lam_re=f(s5_lam_re)[0], lam_im=f(s5_lam_im)[0], log_dt=f(s5_log_dt).reshape(1, 128),
        b_re=f(s5_b_re)[0].reshape(128, 1024), b_im=f(s5_b_im)[0].reshape(128, 1024),
        c_re=f(s5_c_re)[0].reshape(128, 1024), c_im=f(s5_c_im)[0].reshape(128, 1024),
        s5_d=f(np.asarray(s5_d)[0].reshape(16, 128).T), w_glu=f(w_glu)[0], b_glu=f(np.asarray(b_glu)[0].reshape(16, 128).T),
    )
    shared.update(_consts())
    xp = f(x_prompt); xs = f(x_sample); mp = f(mem_prompt)
    ck = f(cache_mem_k).reshape(4, 128, MEM, 512); cvv = f(cache_mem_v).reshape(4, 128, MEM, 512)
    sc = f(state_conv)[0]; sr = f(state_s5_re)[0]; si = f(state_s5_im)[0]
    in_maps = []
    for c in range(8):
        b0, b1 = c * NSB, (c + 1) * NSB
        m = dict(shared)
        m.update(xp=xp[c], xs=xs[b0:b1].reshape(NS, D), mem=mp[c],
                 ck=np.ascontiguousarray(ck[:, b0:b1]), cv=np.ascontiguousarray(cvv[:, b0:b1]),
                 sconv=np.ascontiguousarray(sc[b0:b1].reshape(NSB * 2, E)),
                 s5r0=np.ascontiguousarray(sr[b0:b1]), s5i0=np.ascontiguousarray(si[b0:b1]))
        in_maps.append(m)
    if _CACHE.get("dbg_cores"):
        cores = _CACHE["dbg_cores"]
        res = run_bass_kernel_spmd(nc, [in_maps[c] for c in cores], core_ids=list(range(len(cores))))
        return res.results
    res = run_bass_kernel_spmd(nc, in_maps, core_ids=list(range(8)))
    R = res.results
    cat = lambda k: np.stack([np.asarray(r[k], dtype=np.float32) for r in R])
    y_prompt = cat("y_p")
    y_sample = cat("y_s").reshape(128, 4, D)
    mk = cat("mk_o").transpose(1, 0, 2, 3).reshape(4, 8, MEM, 4, 128)
    mv = cat("mv_o").transpose(1, 0, 2, 3).reshape(4, 8, MEM, 4, 128)
    conv_p = cat("conv_p")[None]
    conv_s = cat("conv_s").reshape(1, 128, 2, E)
    s5rp = cat("s5r_p")[None]; s5ip = cat("s5i_p")[None]
    s5rs = cat("s5r_s").reshape(1, 128, 128, 64); s5is = cat("s5i_s").reshape(1, 128, 128, 64)
    cvs = cat("cv_s").transpose(1, 0, 2, 3).reshape(2, 128, 4, E)
    return (y_prompt, y_sample, np.ascontiguousarray(mk), np.ascontiguousarray(mv), conv_p, conv_s,
            s5rp, s5ip, s5rs, s5is, np.ascontiguousarray(cvs))
```
